# Optimizing a Trainium2 kernel written in Bass

```python
import math, functools
import jax, jax.numpy as jnp
from jax import lax
import numpy as np

D_MODEL = 1024
BATCH = 8
SEQ = 4096
DEPTH = 1
DEC_BATCH = 32
DEC_SEQ = 32
PAST_LEN = 4096

CHUNK = 64
Q_BLOCK = 128
SB_HEADS = 8
SB_HEAD_DIM = 64
SB_WIDTH = SB_HEADS * SB_HEAD_DIM
SB_SCALE = SB_HEAD_DIM ** -0.5
HG_HEADS = 4
HG_KEY_DIM = 128
HG_VAL_DIM = 128
HG_KEY_WIDTH = HG_HEADS * HG_KEY_DIM
HG_VAL_WIDTH = HG_HEADS * HG_VAL_DIM
MIX_WIDTH = SB_WIDTH + HG_VAL_WIDTH
IN_WIDTHS = (SB_WIDTH, SB_WIDTH, SB_WIDTH, HG_KEY_WIDTH, HG_KEY_WIDTH, HG_VAL_WIDTH, HG_VAL_WIDTH)
IN_WIDTH = sum(IN_WIDTHS)
IN_SPLIT_POINTS = tuple(int(v) for v in np.cumsum(IN_WIDTHS)[:-1])
D_FF = 2816
EPS = 1e-6

kernel_name = "stickbreak_hgrn2_macaron_stream_step"


def rms_norm(x, g):
    xf = x.astype(jnp.float32)
    y = xf * lax.rsqrt(jnp.mean(xf * xf, axis=-1, keepdims=True) + EPS)
    return (y * g.astype(jnp.float32)).astype(x.dtype)


def swiglu(x, w_in, w_out):
    a, b = jnp.split(x @ w_in, 2, axis=-1)
    return (jax.nn.silu(a) * b) @ w_out


def sb_attend(q, k, v, q_pos, k_pos):
    z = jnp.einsum('bqhd,bkhd->bhqk', q.astype(jnp.float32), k.astype(jnp.float32)) * SB_SCALE
    mask = k_pos[None, :] < q_pos[:, None]
    log_beta = jax.nn.log_sigmoid(z)
    log_keep = jnp.where(mask, jax.nn.log_sigmoid(-z), 0.0)
    tail = lax.cumsum(log_keep, axis=3, reverse=True) - log_keep
    w = jnp.where(mask, jnp.exp(log_beta + tail), 0.0)
    return jnp.einsum('bhqk,bkhd->bqhd', w.astype(v.dtype), v)


def sb_prompt(q, k, v):
    B, T = q.shape[0], q.shape[1]
    n_blocks = T // Q_BLOCK
    k_pos = jnp.arange(T)

    def one_block(i):
        start = i * Q_BLOCK
        qb = lax.dynamic_slice_in_dim(q, start, Q_BLOCK, axis=1)
        return sb_attend(qb, k, v, start + jnp.arange(Q_BLOCK), k_pos)

    out = lax.map(one_block, jnp.arange(n_blocks))
    return out.transpose(1, 0, 2, 3, 4).reshape(B, T, SB_HEADS, SB_HEAD_DIM)


def sb_sample(q, k, v, ck, cv):
    P, T = ck.shape[1], q.shape[1]
    k_all = jnp.concatenate([ck.astype(k.dtype), k], axis=1)
    v_all = jnp.concatenate([cv.astype(v.dtype), v], axis=1)
    return sb_attend(q, k_all, v_all, P + jnp.arange(T), jnp.arange(P + T))


def hgrn2_scan(q, k, v, log_f, S0, chunk):
    B, T, H, dk = q.shape
    dv = v.shape[-1]
    n = T // chunk

    def to_chunks(a):
        return a.reshape(B, n, chunk, H, a.shape[-1]).transpose(1, 0, 3, 2, 4)

    qc, kc, vc, gc = to_chunks(q), to_chunks(k), to_chunks(v), to_chunks(log_f)
    causal = jnp.tril(jnp.ones((chunk, chunk), dtype=bool))

    def step(S, inp):
        qi, ki, vi, gi = inp
        b = jnp.cumsum(gi, axis=2)
        o_inter = jnp.einsum('bhtc,bhcv->bhtv', qi * jnp.exp(b), S)
        diff = b[:, :, :, None, :] - b[:, :, None, :, :]
        decay = jnp.exp(jnp.where(causal[:, :, None], diff, -jnp.inf))
        att = jnp.einsum('bhtc,bhsc,bhtsc->bhts', qi, ki, decay)
        o = o_inter + jnp.einsum('bhts,bhsv->bhtv', att, vi)
        b_last = b[:, :, -1:, :]
        S_new = jnp.exp(b_last[:, :, 0, :])[..., None] * S + jnp.einsum(
            'bhsc,bhsv->bhcv', ki * jnp.exp(b_last - b), vi)
        return S_new, o

    S_fin, o = lax.scan(step, S0.astype(jnp.float32), (qc, kc, vc, gc))
    o = o.transpose(1, 0, 3, 2, 4).reshape(B, T, H, dv)
    return o, S_fin


def _layer(x, ffn1_norm, ffn1_w_in, ffn1_w_out, mix_norm, w_in, sb_q_gain, sb_k_gain,
           lb, sb_out_gain, hg_out_gain, w_out, ffn2_norm, ffn2_w_in, ffn2_w_out,
           sb_mix, S0, hg_chunk):
    B, T, _ = x.shape
    x = x + 0.5 * swiglu(rms_norm(x, ffn1_norm), ffn1_w_in, ffn1_w_out)
    h = rms_norm(x, mix_norm)
    sq, sk, sv, hq, hf, hi, hg = jnp.split(h @ w_in, IN_SPLIT_POINTS, axis=-1)

    def heads(a, n):
        return a.reshape(B, T, n, a.shape[-1] // n)

    sq = rms_norm(heads(sq, SB_HEADS), sb_q_gain)
    sk = rms_norm(heads(sk, SB_HEADS), sb_k_gain)
    sv = heads(sv, SB_HEADS)
    sb_o = rms_norm(sb_mix(sq, sk, sv), sb_out_gain).reshape(B, T, SB_WIDTH)
    hf32 = hf.astype(jnp.float32)
    f = lb + (1.0 - lb) * jax.nn.sigmoid(hf32)
    log_f = jnp.log(f)
    k_in = (1.0 - lb) * jax.nn.sigmoid(-hf32)
    q_h = jax.nn.silu(hq.astype(jnp.float32))
    o_h, S_new = hgrn2_scan(heads(q_h, HG_HEADS), heads(k_in, HG_HEADS),
                            heads(hi.astype(jnp.float32), HG_HEADS), heads(log_f, HG_HEADS),
                            S0, hg_chunk)
    o_h = rms_norm(o_h, hg_out_gain) * jax.nn.silu(heads(hg, HG_HEADS).astype(jnp.float32))
    mix = jnp.concatenate([sb_o, o_h.reshape(B, T, HG_VAL_WIDTH).astype(x.dtype)], axis=-1)
    x = x + mix @ w_out
    x = x + 0.5 * swiglu(rms_norm(x, ffn2_norm), ffn2_w_in, ffn2_w_out)
    return x, sk, sv, S_new.astype(x.dtype)


def setup_inputs(seed: int = 0) -> dict:
    key = jax.random.key(seed)
    ks = jax.random.split(key, 20)

    def nrm(k, shape, scale):
        return jax.random.normal(k, shape, jnp.float32) * scale

    def gain(k, shape):
        return 1.0 + 0.05 * jax.random.normal(k, shape, jnp.float32)

    return {
        "x_prompt": nrm(ks[0], (BATCH, SEQ, D_MODEL), 1.0),
        "x_sample": nrm(ks[1], (DEC_BATCH, DEC_SEQ, D_MODEL), 1.0),
        "cache_sb_k": nrm(ks[2], (DEPTH, DEC_BATCH, PAST_LEN, SB_HEADS, SB_HEAD_DIM), 1.0),
        "cache_sb_v": nrm(ks[3], (DEPTH, DEC_BATCH, PAST_LEN, SB_HEADS, SB_HEAD_DIM), 1.0),
        "state_hgrn": nrm(ks[4], (DEPTH, DEC_BATCH, HG_HEADS, HG_KEY_DIM, HG_VAL_DIM), 0.5),
        "ffn1_norm": gain(ks[5], (DEPTH, D_MODEL)),
        "ffn1_w_in": nrm(ks[6], (DEPTH, D_MODEL, 2 * D_FF), D_MODEL ** -0.5),
        "ffn1_w_out": nrm(ks[7], (DEPTH, D_FF, D_MODEL), D_FF ** -0.5),
        "mix_norm": gain(ks[8], (DEPTH, D_MODEL)),
        "w_in": nrm(ks[9], (DEPTH, D_MODEL, IN_WIDTH), D_MODEL ** -0.5),
        "sb_q_gain": gain(ks[10], (DEPTH, SB_HEAD_DIM)),
        "sb_k_gain": gain(ks[11], (DEPTH, SB_HEAD_DIM)),
        "hg_lb_logits": nrm(ks[12], (DEPTH + 1, HG_KEY_WIDTH), 0.5),
        "sb_out_gain": gain(ks[13], (DEPTH, SB_HEAD_DIM)),
        "hg_out_gain": gain(ks[14], (DEPTH, HG_VAL_DIM)),
        "w_out": nrm(ks[15], (DEPTH, MIX_WIDTH, D_MODEL), MIX_WIDTH ** -0.5),
        "ffn2_norm": gain(ks[16], (DEPTH, D_MODEL)),
        "ffn2_w_in": nrm(ks[17], (DEPTH, D_MODEL, 2 * D_FF), D_MODEL ** -0.5),
        "ffn2_w_out": nrm(ks[18], (DEPTH, D_FF, D_MODEL), D_FF ** -0.5),
    }


def reference(x_prompt, x_sample, cache_sb_k, cache_sb_v, state_hgrn,
              ffn1_norm, ffn1_w_in, ffn1_w_out, mix_norm, w_in, sb_q_gain, sb_k_gain,
              hg_lb_logits, sb_out_gain, hg_out_gain, w_out, ffn2_norm, ffn2_w_in, ffn2_w_out):
    lb_all = jnp.cumsum(jax.nn.softmax(hg_lb_logits.astype(jnp.float32), axis=0), axis=0)

    xp, xs = x_prompt, x_sample
    kp_list, vp_list, sp_list, ks_list, vs_list, ss_list = [], [], [], [], [], []
    for l in range(DEPTH):
        w_l = (ffn1_norm[l], ffn1_w_in[l], ffn1_w_out[l], mix_norm[l], w_in[l],
               sb_q_gain[l], sb_k_gain[l], lb_all[l], sb_out_gain[l], hg_out_gain[l],
               w_out[l], ffn2_norm[l], ffn2_w_in[l], ffn2_w_out[l])
        S0_p = jnp.zeros((xp.shape[0], HG_HEADS, HG_KEY_DIM, HG_VAL_DIM), jnp.float32)
        xp, kp, vp, sp = _layer(xp, *w_l, sb_mix=sb_prompt, S0=S0_p, hg_chunk=CHUNK)
        sb_mix_s = functools.partial(sb_sample, ck=cache_sb_k[l], cv=cache_sb_v[l])
        xs, kn, vn, sn = _layer(xs, *w_l, sb_mix=sb_mix_s, S0=state_hgrn[l],
                                hg_chunk=xs.shape[1])
        kp_list.append(kp); vp_list.append(vp); sp_list.append(sp)
        ks_list.append(kn); vs_list.append(vn); ss_list.append(sn)

    new_k_prompt = jnp.stack(kp_list, axis=0)
    new_v_prompt = jnp.stack(vp_list, axis=0)
    new_state_prompt = jnp.stack(sp_list, axis=0)
    new_k_sample = jnp.stack(ks_list, axis=0)
    new_v_sample = jnp.stack(vs_list, axis=0)
    new_state_sample = jnp.stack(ss_list, axis=0)
    return (xp, xs, new_k_prompt, new_v_prompt, new_state_prompt,
            new_k_sample, new_v_sample, new_state_sample)
```

```python
import contextlib
import numpy as np
import concourse.bass as bass
import concourse.mybir as mybir
from concourse.bass_utils import run_bass_kernel_spmd

F32 = mybir.dt.float32
BF16 = mybir.dt.bfloat16
AF = mybir.ActivationFunctionType
ALU = mybir.AluOpType
AX = mybir.AxisListType

D = 1024
KC = 8
DFF = 2816
NFC = 22
EPS = 1e-6
SB_SCALE = 0.125
NR = 5
NPG = 11

ENGS = ("pe", "act", "dve", "pool", "sp")
SAME_ENGINE_SYNC = {"pe": False, "act": True, "dve": True, "pool": True, "sp": False}


class Op:
    __slots__ = ("eng", "fn", "deps", "needs_inc", "semval", "dkey", "dval")


class Prog:
    def __init__(self):
        self.ops = {e: [] for e in ENGS}
        self.lw = {}
        self.rd = {}
        self.dcount = {}
        self.dlast = {}

    def op(self, eng, fn, reads=(), writes=(), dma=None):
        o = Op()
        o.eng, o.fn, o.needs_inc, o.semval = eng, fn, False, None
        o.dkey, o.dval = None, None
        writes = list(writes) + [r for r in reads if isinstance(r, tuple) and r[0] == "ps" and r not in writes]
        deps = []
        for r in reads:
            t = self.lw.get(r)
            if t is not None:
                deps.append(t)
        for w in writes:
            t = self.lw.get(w)
            if t is not None:
                deps.append(t)
            for t in self.rd.get(w, {}).values():
                deps.append(t)
        if dma is not None:
            t = self.dlast.get(dma)
            if t is not None:
                deps.append(t)
            self.dcount[dma] = self.dcount.get(dma, 0) + 16
            o.dkey, o.dval = dma, self.dcount[dma]
            self.dlast[dma] = o
        o.deps = deps
        for t in deps:
            if t.dkey is None:
                t.needs_inc = True
        for w in writes:
            self.lw[w] = o
            self.rd[w] = {}
        for r in reads:
            if r in writes:
                continue
            self.rd.setdefault(r, {})[dma if dma is not None else eng] = o
        self.ops[eng].append(o)
        return o

    def emit(self, block, esem, dsem):
        for e in ENGS:
            c = 0
            for o in self.ops[e]:
                if o.dkey is None and o.needs_inc:
                    c += 1
                    o.semval = c
        prog = self

        def mk(e):
            def body(engine):
                seen = {}
                for o in prog.ops[e]:
                    for t in o.deps:
                        if t.dkey is not None:
                            sem, val, key = dsem[t.dkey], t.dval, ("d", t.dkey)
                        else:
                            if t.eng == e and not SAME_ENGINE_SYNC[e]:
                                continue
                            sem, val, key = esem[t.eng], t.semval, ("e", t.eng)
                        if seen.get(key, 0) < val:
                            engine.wait_ge(sem, val)
                            seen[key] = val
                    inst = o.fn(engine)
                    if o.dkey is not None:
                        inst.then_inc(dsem[o.dkey], 16)
                    elif o.needs_inc:
                        inst.then_inc(esem[e], 1)
                if e == "sp":
                    for k, v in prog.dcount.items():
                        if seen.get(("d", k), 0) < v:
                            engine.wait_ge(dsem[k], v)
            return body

        block.tensor(mk("pe"))
        block.scalar(mk("act"))
        block.vector(mk("dve"))
        block.gpsimd(mk("pool"))
        block.sync(mk("sp"))


def plan_tiles(npb):
    tot = npb + 1
    nt = -(-tot // 4)
    base, extra = divmod(tot, nt)
    sizes = [base + 1] * extra + [base] * (nt - extra)
    tiles = []
    g = 0
    for i, s in enumerate(sizes):
        last = i == nt - 1
        n_p = s - 1 if last else s
        tiles.append((g, n_p, last))
        g += n_p
    assert g == npb
    return tiles


SLOTS = ([("f1u", g) for g in range(11)] + [("f1d", g) for g in range(6)] +
         [("in", c) for c in (3, 4, 6, 5, 0, 1, 2)] + [("out", g) for g in range(2)] +
         [("f2u", g) for g in range(11)] + [("f2d", g) for g in range(6)])
SLOTPOS = {s: i for i, s in enumerate(SLOTS)}
NSLOT = len(SLOTS)


class Builder:
    def __init__(self, npb, pastb):
        self.npb, self.pastb = npb, pastb
        self.tiles = plan_tiles(npb)
        self.nc = bass.Bass("TRN2", target_bir_lowering=False)
        self.P = Prog()
        self.bank_rr = 0
        self.issued = 0

    def declare(self, es):
        nc = self.nc
        S, PAST = self.npb * 128, self.pastb * 128

        def din(name, shape):
            return nc.dram_tensor(name, shape, F32, kind="ExternalInput").ap()

        def dout(name, shape):
            return nc.dram_tensor(name, shape, F32, kind="ExternalOutput").ap()

        self.xp = din("xp", [S, D])
        self.xs = din("xs", [128, D])
        self.ck = din("ck", [4, PAST, 512])
        self.cv = din("cv", [4, PAST, 512])
        self.s0 = din("s0", [4, 4, 128, 128])
        self.wm = {"f1u": din("f1wi", [D, 2 * DFF]), "f1d": din("f1wo", [DFF, D]),
                   "in": din("win", [D, 3584]), "out": din("wout", [D, D]),
                   "f2u": din("f2wi", [D, 2 * DFF]), "f2d": din("f2wo", [DFF, D])}
        self.vrows = din("vrows", [34, 128])
        self.qkg = din("qkg", [2, 64])
        self.yp = dout("yp", [S, D])
        self.ys = dout("ys", [128, D])
        self.kp = dout("kp", [S, 512])
        self.vp = dout("vp", [S, 512])
        self.stp = dout("stp", [4, 128, 128])
        self.ks = dout("ks", [128, 512])
        self.vs = dout("vs", [128, 512])
        self.sts = dout("sts", [4, 4, 128, 128])
        self.wsc = nc.dram_tensor("wsc", [NSLOT, 128, 4096], BF16, kind="Internal").ap()

        def sb(name, shape, dt):
            return es.enter_context(nc.sbuf_tensor(name, shape, dt))

        self.xt = sb("xt", [128, 4, D], F32)
        self.hT = sb("hT", [128, 8, 512], BF16)
        self.KT = sb("KT", [128, 4, max(self.npb, self.pastb) * 128], BF16)
        self.V = sb("V", [128, max(self.npb, self.pastb), 512], BF16)
        self.wr = sb("wr", [128, NR, 4096], BF16)
        self.pgb = sb("pg", [128, NPG, 1024], BF16)
        self.sgT = sb("sgT", [128, 4, 512], F32)
        self.qhT = sb("qhT", [128, 4, 512], BF16)
        self.gT = sb("gT", [128, 4, 512], BF16)
        self.qz = sb("qz", [128, 2, 4, 512], BF16)
        self.vh = sb("vh", [128, 4, 512], BF16)
        self.hbuf = sb("hbuf", [128, 4, 5, 128], BF16)
        self.kTs = sb("kTs", [128, 4, 128], BF16)
        self.vsb = sb("vsb", [128, 512], BF16)
        self.qbd = sb("qbd", [128, 4, 4, 64], BF16)
        self.identb = sb("identb", [128, 128], BF16)
        self.identf = sb("identf", [128, 128], F32)
        self.onesf = sb("onesf", [128, 256], F32)
        self.triN = sb("triN", [128, 128], BF16)
        self.othN = sb("othN", [128, 128], BF16)
        self.onesN = sb("onesN", [128, 128], BF16)
        self.cmaskA = sb("cmaskA", [128, 128], F32)
        self.hmask64 = sb("hmask64", [128, 128], F32)
        self.hmask32 = sb("hmask32", [128, 128], F32)
        self.onesbd = sb("onesbd", [128, 128], BF16)
        self.onesfu = sb("onesfu", [128, 128], BF16)
        self.zeros = sb("zeros", [128, 512], BF16)
        self.smask = sb("smask", [128, 4, 256], F32)
        self.rmp = sb("rmp", [128, 512], F32)
        self.rms = sb("rms", [128, 128], F32)
        self.rm64 = sb("rm64", [128, 2], F32)
        self.rm32 = sb("rm32", [128, 4], F32)
        self.gq = sb("gq", [128, 64], F32)
        self.gk = sb("gk", [128, 64], F32)
        self.vr = sb("vr", [34, 128], F32)
        self.cols = sb("cols", [128, 40], F32)
        self.negh = sb("negh", [128, 1], F32)
        self.epsc = sb("epsc", [128, 1], F32)
        self.Sst = sb("Sst", [128, 4, 128], F32)
        self.Sbf = sb("Sbf", [128, 4, 2, 128], BF16)
        self.dec = sb("dec", [128, 4, 8], F32)
        self.decs = sb("decs", [128, 4, 4], F32)
        self.ssq = sb("ssq", [128, 8], F32)
        self.rstd = sb("rstd", [128, 8], F32)
        self.ss8 = sb("ss8", [128, 2, 8], F32)
        self.rs8 = sb("rs8", [128, 2, 8], F32)
        self.banks = [es.enter_context(nc.psum_tensor("bk%d" % i, [128, 512], F32)) for i in range(8)]

    def pf(self, i):
        return self.pgb[:, i, :].bitcast(F32)

    def pb(self, i, h):
        return self.pgb[:, i, h * 512:(h + 1) * 512]

    def bkk(self, bk):
        return [("ps", bk)]

    def bks(self, bk, *subs):
        return [("ps", bk)]

    def hk(self, i, h):
        return [("act", 2 * i + h)]

    def pgw(self, i):
        return [("act", 2 * i), ("act", 2 * i + 1)]

    def bkf(self, i):
        return self.banks[i][:, :]

    def bkb(self, i):
        return self.banks[i][:, :].bitcast(BF16)

    def nextbank(self):
        b = self.bank_rr
        self.bank_rr = (self.bank_rr + 1) % 8
        return b

    def mm(self, out, lhsT, rhs, start, stop, reads, writes):
        self.P.op("pe", lambda e: e.matmul(out, lhsT=lhsT, rhs=rhs, start=start, stop=stop),
                  reads=reads, writes=writes)

    def tr(self, out, in_, ident, reads, writes):
        self.P.op("pe", lambda e: e.transpose(out=out, in_=in_, identity=ident), reads=reads, writes=writes)

    def act(self, out, in_, func, reads, writes, **kw):
        self.P.op("act", lambda e: e.activation(out=out, in_=in_, func=func, **kw), reads=reads, writes=writes)

    def ts(self, eng, out, in0, s1, s2, op0, op1, reads, writes):
        if op1 is None:
            self.P.op(eng, lambda e: e.tensor_scalar(out=out, in0=in0, scalar1=s1, scalar2=None, op0=op0),
                      reads=reads, writes=writes)
        else:
            self.P.op(eng, lambda e: e.tensor_scalar(out=out, in0=in0, scalar1=s1, scalar2=s2, op0=op0, op1=op1),
                      reads=reads, writes=writes)

    def tt(self, eng, out, in0, in1, op, reads, writes):
        self.P.op(eng, lambda e: e.tensor_tensor(out=out, in0=in0, in1=in1, op=op), reads=reads, writes=writes)

    def stt(self, out, in0, scalar, in1, op0, op1, reads, writes):
        self.P.op("dve", lambda e: e.scalar_tensor_tensor(out=out, in0=in0, scalar=scalar, in1=in1, op0=op0, op1=op1),
                  reads=reads, writes=writes)

    def cp(self, eng, out, in_, reads, writes):
        if eng == "act":
            self.act(out, in_, AF.Copy, reads, writes)
        else:
            self.P.op(eng, lambda e: e.tensor_copy(out=out, in_=in_), reads=reads, writes=writes)

    def dma(self, eng, out, in_, reads, writes, key):
        self.P.op(eng, lambda e: e.dma_start(out=out, in_=in_), reads=reads, writes=writes, dma=key)

    def pow_rstd(self, out, in_, n, reads, writes):
        nh = self.negh[:, 0:1] if n == 1 else self.negh[:, 0:1].to_broadcast([128, n])
        self.tt("pool", out, in_, nh, ALU.pow, list(reads) + ["negh"], writes)

    def setup(self):
        P = self.P

        def pool(fn, reads=(), writes=()):
            P.op("pool", fn, reads=reads, writes=writes)

        def sel(out, in_, pattern, cmp, base, cm, reads, writes):
            pool(lambda e: e.affine_select(out=out, in_=in_, pattern=pattern, compare_op=cmp, fill=0.0,
                                           base=base, channel_multiplier=cm), reads, writes)

        pool(lambda e: e.memset(self.onesf[:], 1.0), writes=["onesf"])
        pool(lambda e: e.memset(self.negh[:], -0.5), writes=["negh"])
        pool(lambda e: e.memset(self.epsc[:], EPS), writes=["epsc"])
        pool(lambda e: e.memset(self.zeros[:], 0.0), writes=["zeros"])
        pool(lambda e: e.memset(self.qbd[:], 0.0), writes=["qbd"])
        pool(lambda e: e.memset(self.qz[:], 0.0), writes=[("qT", b) for b in range(4)])
        pool(lambda e: e.memset(self.onesfu[:], 1.0), writes=["onesfu"])
        pool(lambda e: e.memset(self.onesN[:], -1.0), writes=["onesN"])
        pool(lambda e: e.memset(self.Sst[:], 0.0), writes=[("S", h) for h in range(4)])
        of = self.onesf[:, 0:128]
        sel(self.identf[:], of, [[-1, 128]], ALU.is_equal, 0, 1, ["onesf"], ["identf"])
        sel(self.identb[:], of, [[-1, 128]], ALU.is_equal, 0, 1, ["onesf"], ["identb"])
        sel(self.cmaskA[:], of, [[-1, 128]], ALU.is_ge, 0, 1, ["onesf"], ["cmaskA"])
        pool(lambda e: e.tensor_scalar(out=self.triN[:], in0=self.cmaskA[:], scalar1=-1.0, scalar2=None, op0=ALU.mult),
             ["cmaskA"], ["triN"])
        sel(self.hmask64[:], of, [[1, 128]], ALU.is_gt, 0, -1, ["onesf"], ["hmask64"])
        pool(lambda e: e.tensor_scalar(out=self.othN[:], in0=self.hmask64[:], scalar1=-1.0, scalar2=None, op0=ALU.mult),
             ["hmask64"], ["othN"])
        sel(self.cmaskA[:], of, [[1, 128]], ALU.is_gt, 0, -1, ["onesf", "triN"], ["cmaskA"])
        sel(self.hmask64[:], of, [[1, 128]], ALU.is_ge, 0, -1, ["onesf", "othN"], ["hmask64"])
        sel(self.hmask32[:], of, [[1, 128]], ALU.is_ge, 0, -1, ["onesf"], ["hmask32"])
        pool(lambda e: e.memset(self.hmask64[0:64, 64:128], 0.0), ["hmask64"], ["hmask64"])
        for i in range(3):
            pool(lambda e, i=i: e.memset(self.hmask32[32 * i:32 * i + 32, 32 * (i + 1):128], 0.0), ["hmask32"], ["hmask32"])
        pool(lambda e: e.memset(self.onesbd[:], 0.0), writes=["onesbd"])
        pool(lambda e: e.memset(self.onesbd[0:64, 0:64], 1.0), ["onesbd"], ["onesbd"])
        pool(lambda e: e.memset(self.onesbd[64:128, 64:128], 1.0), ["onesbd"], ["onesbd"])
        for C, rm in ((64, self.rm64), (32, self.rm32)):
            for j in range(128 // C):
                sel(rm[:, j:j + 1], self.onesf[:, 0:1], [[0, 1]], ALU.is_ge, -j * C, 1, ["onesf"], ["rm%d" % C])
                sel(rm[:, j:j + 1], rm[:, j:j + 1], [[0, 1]], ALU.is_ge, (j + 1) * C - 1, -1, ["rm%d" % C], ["rm%d" % C])
        for j in range(4):
            sel(self.smask[:, j, :], self.onesf[:, 0:256], [[0, 8], [1, 32]], ALU.is_gt, 32 * j, -1, ["onesf"], ["smask"])
            sel(self.smask[:, j, :], self.smask[:, j, :], [[0, 256]], ALU.is_ge, -32 * j, 1, ["smask"], ["smask"])
        pool(lambda e: e.memset(self.rmp[:], 1.0), writes=["rmp"])
        pool(lambda e: e.memset(self.rmp[:].rearrange("p (a b) -> p a b", b=64)[:, :, 0:1], 0.0), ["rmp"], ["rmp"])
        pool(lambda e: e.memset(self.rms[:], 1.0), writes=["rms"])
        pool(lambda e: e.memset(self.rms[:].rearrange("p (a b) -> p a b", b=32)[:, :, 0:1], 0.0), ["rms"], ["rms"])
        self.dma("sp", self.vr[:], self.vrows, [], ["vr"], "ldv")
        self.dma("sp", self.gq[:], self.qkg[0:1, :].to_broadcast([128, 64]),
                 [], ["gq"], "ldgq")
        self.dma("sp", self.gk[:], self.qkg[1:2, :].to_broadcast([128, 64]), [], ["gk"], "ldgk")
        bk = self.nextbank()
        self.tr(self.bkf(bk)[:, 0:34], self.vr[:], self.identf[0:34, 0:34], ["vr", "identf"], self.bkk(bk))
        self.cp("dve", self.cols[:, 0:34], self.bkf(bk)[:, 0:34], self.bkk(bk), ["cols"])
        c = self.cols
        self.tt("dve", c[:, 34:38], c[:, 28:32], c[:, 24:28], ALU.subtract, ["cols"], ["cols"])
        self.act(c[:, 34:38], c[:, 34:38], AF.Exp, ["cols"], ["cols"])
        self.ts("dve", c[:, 34:38], c[:, 34:38], 1.0, None, ALU.add, None, ["cols"], ["cols"])
        self.P.op("dve", lambda e: e.reciprocal(out=c[:, 34:38], in_=c[:, 34:38]), reads=["cols"], writes=["cols"])
        self.ts("dve", c[:, 24:28], c[:, 34:38], 0.5, 0.5, ALU.mult, ALU.add, ["cols"], ["cols"])
        self.ts("dve", c[:, 28:32], c[:, 34:38], -0.5, 0.5, ALU.mult, ALU.add, ["cols"], ["cols"])
        self.ts("dve", c[:, 34:38], c[:, 28:32], -1.0, None, ALU.mult, None, ["cols"], ["cols"])

    def wview(self, r, kind):
        if kind == "cols":
            return self.wr[:, r, :].rearrange("p (k c) -> p k c", k=8)
        return self.wr[:, r, :].rearrange("p (c n) -> p c n", c=4)

    def issue_load(self, u):
        ti, pos = divmod(u, NSLOT)
        r = u % NR
        name, g = SLOTS[pos]
        if ti == 0:
            W = self.wm[name]
            if name in ("f1u", "f2u", "in"):
                src = W[:, g * 512:(g + 1) * 512].rearrange("(k p) c -> p k c", p=128)
                dst = self.wview(r, "cols")
            else:
                R = W.shape[0]
                n = min(4, R // 128 - 4 * g)
                src = W[g * 512:g * 512 + n * 128, :].rearrange("(c p) n -> p c n", p=128)
                dst = self.wview(r, "rows")[:, 0:n, :]
            self.dma("pool", dst, src, [], [("w", r)], ("wlc", r))
            if len(self.tiles) > 1:
                self.dma("sp", self.wsc[pos], self.wr[:, r, :], [("w", r)], [("wsc", pos)], ("wst", r))
        else:
            self.dma("sp", self.wr[:, r, :], self.wsc[pos], [("wsc", pos)], [("w", r)], ("wl", r))

    def use(self, ti, slot, oldest=None):
        u = ti * NSLOT + SLOTPOS[slot]
        base = u if oldest is None else ti * NSLOT + SLOTPOS[oldest]
        lim = min(base + NR - 1, len(self.tiles) * NSLOT - 1)
        while self.issued <= lim:
            self.issue_load(self.issued)
            self.issued += 1
        assert self.issued > u
        return u % NR

    def rmsnorm(self, T, gcol0):
        nb = T["nb"]
        for b in range(nb):
            xk = ("xt", b)
            self.P.op("dve", lambda e, b=b: e.memset(self.ssq[:, b:b + 1], 0.0), writes=[("ssq", b)])
            self.act(self.pgb[:, b, :], self.xt[:, b, :], AF.Square, [xk, ("ssq", b)], self.pgw(b) + [("ssq", b)],
                     accum_out=self.ssq[:, b:b + 1])
            self.ts("dve", self.rstd[:, b:b + 1], self.ssq[:, b:b + 1], 1.0 / D, EPS, ALU.mult, ALU.add,
                    [("ssq", b)], [("rstd", b)])
            self.pow_rstd(self.rstd[:, b:b + 1], self.rstd[:, b:b + 1], 1, [("rstd", b)], [("rstd", b)])
        for b in range(nb):
            if b % 2 == 0:
                self.ts("dve", self.pgb[:, b, :], self.xt[:, b, :], self.rstd[:, b:b + 1], None,
                        ALU.mult, None, [("xt", b), ("rstd", b)], self.pgw(b))
            else:
                self.act(self.pgb[:, b, :], self.xt[:, b, :], AF.Copy, [("xt", b), ("rstd", b)], self.pgw(b),
                         scale=self.rstd[:, b:b + 1])
        bks = []
        for b in range(nb):
            bk = self.nextbank()
            bks.append(bk)
            xn = self.pgb[:, b, :]
            for k in range(8):
                self.tr(self.bkb(bk)[:, k * 128:(k + 1) * 128], xn[:, k * 128:(k + 1) * 128], self.identb[:],
                        self.pgw(b) + ["identb"], self.bkk(bk))
        for b in range(nb):
            bk = bks[b]
            self.tt("dve", self.hT[:, :, b * 128:(b + 1) * 128],
                    self.bkb(bk)[:, 0:1024].rearrange("p (k c) -> p k c", k=8),
                    self.cols[:, gcol0:gcol0 + 8].unsqueeze(2).to_broadcast([128, 8, 128]), ALU.mult,
                    self.bkk(bk) + ["cols"], [("hT", b)])

    def ffn(self, T, which):
        ti, nb, n = T["ti"], T["nb"], T["n"]
        up, dn = ("f1u", "f1d") if which == 1 else ("f2u", "f2d")
        self.rmsnorm(T, 0 if which == 1 else 16)
        hk = [("hT", b) for b in range(nb)]
        for mc in range(44):
            r = self.use(ti, (up, mc // 4))
            w = self.wview(r, "cols")
            bk = self.nextbank()
            for k in range(8):
                self.mm(self.bkf(bk)[:, 0:n], w[:, k, (mc % 4) * 128:(mc % 4) * 128 + 128], self.hT[:, k, 0:n],
                        k == 0, k == 7, [("w", r)] + hk, self.bkk(bk))
            c = mc % 22
            a = self.pb(c // 2, c % 2)[:, 0:n]
            if mc < 22:
                self.act(a, self.bkf(bk)[:, 0:n], AF.Silu, self.bkk(bk), [("act", c)])
            else:
                self.tt("dve", a, self.bkf(bk)[:, 0:n], a, ALU.mult, self.bkk(bk) + [("act", c)], [("act", c)])
        for grp in range(3):
            rs = {}
            rs[2 * grp] = self.use(ti, (dn, 2 * grp))
            rs[2 * grp + 1] = self.use(ti, (dn, 2 * grp + 1), oldest=(dn, 2 * grp))
            chunks = [c for c in range(8 * grp, min(8 * grp + 8, NFC))]
            for b in range(nb):
                for nh in range(2):
                    bk = self.nextbank()
                    for i, c in enumerate(chunks):
                        r = rs[c // 4]
                        w = self.wview(r, "rows")
                        self.mm(self.bkf(bk)[:, :], self.pb(c // 2, c % 2)[:, b * 128:(b + 1) * 128],
                                w[:, c % 4, nh * 512:(nh + 1) * 512], i == 0, i == len(chunks) - 1,
                                [("w", r), ("act", c)], self.bkk(bk))
                    xs_ = self.xt[:, b, nh * 512:(nh + 1) * 512]
                    self.stt(xs_, self.bkf(bk)[:, :], 0.5, xs_, ALU.mult, ALU.add, self.bkk(bk) + [("xt", b)], [("xt", b)])

    def inproj(self, T):
        ti, nb, n = T["ti"], T["nb"], T["n"]
        self.rmsnorm(T, 8)
        hk = [("hT", b) for b in range(nb)]
        for (cg, kind) in ((3, "hq"), (4, "hf"), (6, "hg")):
            r = self.use(ti, ("in", cg))
            w = self.wview(r, "cols")
            for hh in range(4):
                bk = self.nextbank()
                for k in range(8):
                    self.mm(self.bkf(bk)[:, 0:n], w[:, k, hh * 128:(hh + 1) * 128], self.hT[:, k, 0:n], k == 0, k == 7,
                            [("w", r)] + hk, self.bkk(bk))
                if kind == "hq":
                    self.act(self.qhT[:, hh, 0:n], self.bkf(bk)[:, 0:n], AF.Silu, self.bkk(bk), [("qhT", hh)])
                elif kind == "hf":
                    self.act(self.sgT[:, hh, 0:n], self.bkf(bk)[:, 0:n], AF.Tanh, self.bkk(bk), [("sgT", hh)], scale=0.5)
                else:
                    self.act(self.gT[:, hh, 0:n], self.bkf(bk)[:, 0:n], AF.Silu, self.bkk(bk), [("gT", hh)])
        r = self.use(ti, ("in", 5))
        w = self.wview(r, "cols")
        for b in range(nb):
            bk = self.nextbank()
            for k in range(8):
                self.mm(self.bkf(bk)[:, :], self.hT[:, k, b * 128:(b + 1) * 128], w[:, k, :], k == 0, k == 7,
                        [("w", r), ("hT", b)], self.bkk(bk))
            self.cp("dve", self.vh[:, b, :], self.bkf(bk)[:, :], self.bkk(bk), [("vh", b)])

    def qkv_begin(self, T):
        ti = T["ti"]
        self.qkv_r = {"q": self.use(ti, ("in", 0)), "k": self.use(ti, ("in", 1), oldest=("in", 0)),
                      "v": self.use(ti, ("in", 2), oldest=("in", 0))}
        self.qkv_pending = None
        self.qkv_n = 0

    def qkv_flush(self):
        if self.qkv_pending is not None:
            self.qkv_pending()
            self.qkv_pending = None

    def qkv_unit(self, T, kind, b):
        nb, gb0 = T["nb"], T["gb0"]
        h8 = "p (h d) -> p h d"
        r = self.qkv_r[kind]
        w = self.wview(r, "cols")
        samp = T["samp"] and b == nb - 1
        gb = gb0 + b
        bk = 7
        for k in range(8):
            self.mm(self.bkf(bk)[:, :], self.hT[:, k, b * 128:(b + 1) * 128], w[:, k, :], k == 0, k == 7,
                    [("w", r), ("hT", b)], self.bkk(bk))
        par = self.qkv_n % 2
        self.qkv_n += 1
        pA, pB, pC, pD, pE = 6, 7, 8, 9, 10
        psk = self.bkk(bk)
        if kind == "v":
            vf = self.pf(pE)
            self.cp("act", vf, self.bkf(bk)[:, :], psk, self.pgw(pE))
            dst = self.vs if samp else self.vp[gb * 128:(gb + 1) * 128, :]
            self.dma("pool", dst, vf, self.pgw(pE), [], ("stv", 0))
            if samp:
                self.cp("dve", self.vsb[:, :], vf, self.pgw(pE), ["vsb"])
            else:
                self.cp("dve", self.V[:, gb, :], vf, self.pgw(pE), [("V", gb)])
            return
        sq = self.pf(pA)
        self.act(sq, self.bkf(bk)[:, :], AF.Square, psk, self.pgw(pA))
        self.P.op("dve", lambda e, sq=sq, par=par: e.tensor_reduce(
            out=self.ss8[:, par, :], in_=sq.rearrange(h8, h=8), axis=AX.X, op=ALU.add),
            reads=self.pgw(pA), writes=[("ss8", par)])
        self.ts("dve", self.rs8[:, par, :], self.ss8[:, par, :], 1.0 / 64, EPS, ALU.mult, ALU.add,
                [("ss8", par)], [("rs8", par)])
        self.pow_rstd(self.rs8[:, par, :], self.rs8[:, par, :], 8, [("rs8", par)], [("rs8", par)])
        qn = self.pf(pB)
        self.tt("dve", qn.rearrange(h8, h=8), self.bkf(bk)[:, :].rearrange(h8, h=8),
                self.rs8[:, par, :].unsqueeze(2).to_broadcast([128, 8, 64]), ALU.mult,
                psk + [("rs8", par)], self.pgw(pB))
        gain = self.gq if kind == "q" else self.gk
        gkey = "gq" if kind == "q" else "gk"
        gb_ = gain[:, :].unsqueeze(1).to_broadcast([128, 8, 64])
        if kind == "q":
            src, srck = self.pb(pC, 0), self.hk(pC, 0)
            self.tt("dve", src.rearrange(h8, h=8), qn.rearrange(h8, h=8), gb_, ALU.mult, self.pgw(pB) + [gkey], srck)
        else:
            kf = self.pf(pD)
            self.tt("dve", kf.rearrange(h8, h=8), qn.rearrange(h8, h=8), gb_, ALU.mult, self.pgw(pB) + [gkey], self.pgw(pD))
            src, srck = self.pb(pC, 1), self.hk(pC, 1)
            self.cp("act", src, kf, self.pgw(pD), srck)

        def fin(src=src, srck=srck, kind=kind, b=b, gb=gb, samp=samp, pD=pD):
            if kind == "k":
                dst = self.ks if samp else self.kp[gb * 128:(gb + 1) * 128, :]
                self.dma("pool", dst, self.pf(pD), self.pgw(pD), [], ("stk", 0))
            bk2 = 1
            for pr in range(4):
                self.tr(self.bkb(bk2)[:, pr * 128:(pr + 1) * 128], src[:, pr * 128:(pr + 1) * 128], self.identb[:],
                        srck + ["identb"], self.bkk(bk2))
            tin = self.bkb(bk2)[:, 0:512].rearrange("p (r c) -> p r c", r=4)
            if kind == "q":
                for hl in range(2):
                    ps_ = slice(64 * hl, 64 * hl + 64)
                    self.cp("act", self.qz[ps_, hl, :, b * 128:(b + 1) * 128],
                            self.bkb(bk2)[ps_, 0:512].rearrange("p (r c) -> p r c", r=4), self.bkk(bk2), [("qT", b)])
            elif samp:
                self.cp("act", self.kTs[:, :, :], tin, self.bkk(bk2), ["kTs"])
            else:
                self.cp("act", self.KT[:, :, gb * 128:(gb + 1) * 128], tin, self.bkk(bk2), [("KT", gb)])
        self.qkv_flush()
        self.qkv_pending = fin

    def hgrn(self, T):
        nb, n, np_, gb0 = T["nb"], T["n"], T["np"], T["gb0"]
        npc = np_ * 128
        samp = T["samp"]
        nchp = np_ * 2
        c = self.cols
        tf, tk, tb_, teb, tenb = (self.pf(6 + i)[:, 0:n] for i in range(5))
        kf_, kk_, kb_k, keb, kenb = (self.pgw(6 + i) for i in range(5))
        c64 = "p (a b) -> p a b"
        for hh in range(4):
            qt_ = self.pb(hh, 0)[:, 0:n]
            kbt = self.pb(hh, 1)[:, 0:n]
            kht = self.pb(4 + hh // 2, hh % 2)[:, 0:n]
            kq0, kq1, kh = self.hk(hh, 0), self.hk(hh, 1), self.hk(4 + hh // 2, hh % 2)
            self.act(tf, self.sgT[:, hh, 0:n], AF.Ln, [("sgT", hh), "cols"], kf_,
                     scale=c[:, 28 + hh:29 + hh], bias=c[:, 24 + hh:25 + hh])
            self.ts("dve", tk, self.sgT[:, hh, 0:n], c[:, 34 + hh:35 + hh], c[:, 28 + hh:29 + hh], ALU.mult, ALU.add,
                    [("sgT", hh), "cols"], kk_)
            if npc:
                self.P.op("dve", lambda e: e.tensor_tensor_scan(out=tb_[:, 0:npc], data0=self.rmp[:, 0:npc], data1=tf[:, 0:npc],
                                                                initial=0.0, op0=ALU.mult, op1=ALU.add),
                          reads=kf_ + ["rmp"], writes=kb_k)
            if samp:
                self.P.op("dve", lambda e: e.tensor_tensor_scan(out=tb_[:, npc:n], data0=self.rms[:, :], data1=tf[:, npc:n],
                                                                initial=0.0, op0=ALU.mult, op1=ALU.add),
                          reads=kf_ + ["rms"], writes=kb_k)
            self.act(teb, tb_, AF.Exp, kb_k, keb)
            self.act(tenb, tb_, AF.Exp, kb_k, kenb, scale=-1.0)
            self.tt("dve", qt_, self.qhT[:, hh, 0:n], teb, ALU.mult, [("qhT", hh)] + keb, kq0)
            self.tt("dve", tk, tk, tenb, ALU.mult, kk_ + kenb, kk_)
            self.cp("act", kbt, tk, kk_, kq1)
            if npc:
                self.cp("dve", self.dec[:, hh, 0:nchp].unsqueeze(2),
                        teb[:, 0:npc].rearrange(c64, b=64)[:, :, 63:64], keb, [("dec", hh)])
                self.tt("dve", kht[:, 0:npc].rearrange(c64, b=64), tk[:, 0:npc].rearrange(c64, b=64),
                        self.dec[:, hh, 0:nchp].unsqueeze(2).to_broadcast([128, nchp, 64]), ALU.mult,
                        kk_ + [("dec", hh)], kh)
            if samp:
                self.cp("dve", self.decs[:, hh, 0:4].unsqueeze(2),
                        teb[:, npc:n].rearrange(c64, b=32)[:, :, 31:32], keb, [("decs", hh)])
                self.tt("dve", kht[:, npc:n].rearrange(c64, b=32), tk[:, npc:n].rearrange(c64, b=32),
                        self.decs[:, hh, 0:4].unsqueeze(2).to_broadcast([128, 4, 32]), ALU.mult,
                        kk_ + [("decs", hh)], kh)
        BPA, BPK = 0, 1
        self.qkv_begin(T)
        for b in range(nb):
            issamp = samp and b == nb - 1
            if issamp:
                for kind in ("q", "k", "v"):
                    self.qkv_unit(T, kind, b)
                self.qkv_flush()
            C = 32 if issamp else 64
            nch = 128 // C
            bc = slice(b * 128, (b + 1) * 128)
            hm = self.hmask32 if issamp else self.hmask64
            hmk = "hmask32" if issamp else "hmask64"
            rm = self.rm32 if issamp else self.rm64
            gb = gb0 + b
            for hh in range(4):
                qt_ = self.pb(hh, 0)
                kbt = self.pb(hh, 1)
                self.mm(self.bkf(BPA)[:, hh * 128:(hh + 1) * 128], kbt[:, bc], qt_[:, bc], True, True,
                        self.hk(hh, 0) + self.hk(hh, 1), self.bkk(BPA))
            self.tt("dve", self.hbuf[:, :, 0, :], self.bkf(BPA)[:, :].rearrange("p (h t) -> p h t", h=4),
                    hm[:, :].unsqueeze(1).to_broadcast([128, 4, 128]), ALU.mult, self.bkk(BPA) + [hmk],
                    [("hb", hh, 0) for hh in range(4)])
            for hh in range(4):
                kht = self.pb(4 + hh // 2, hh % 2)
                self.tr(self.bkb(BPK)[:, hh * 128:(hh + 1) * 128], kht[:, bc], self.identb[:],
                        self.hk(4 + hh // 2, hh % 2) + ["identb"], self.bkk(BPK))
            for j in range(nch):
                self.ts("dve", self.hbuf[:, :, 1 + j, :], self.bkb(BPK)[:, 0:512].rearrange("p (h t) -> p h t", h=4),
                        rm[:, j:j + 1], None, ALU.mult, None, self.bkk(BPK) + ["rm%d" % C],
                        [("hb", hh, 1 + j) for hh in range(4)])
            if not issamp:
                self.qkv_unit(T, "q", b)
            if issamp:
                for hh in range(4):
                    qt_ = self.pb(hh, 0)
                    kq0 = self.hk(hh, 0)
                    po = self.bkf(2 + hh)
                    pok = self.bkk(2 + hh)
                    vsl = self.vh[:, b, hh * 128:(hh + 1) * 128]
                    stf = self.pf(6 + hh % 2).rearrange("p (s d) -> p s d", s=4)
                    stb = self.pb(8, hh % 2).rearrange("p (s d) -> p s d", s=4)
                    kstf, kstb = self.pgw(6 + hh % 2), self.hk(8, hh % 2)
                    self.dma("sp", stf, self.s0[:, hh, :, :].rearrange("s c d -> c s d"), [], kstf, ("lds", hh % 2))
                    self.cp("act", stb, stf, kstf, kstb)
                    for j in range(4):
                        cs = slice(b * 128 + 32 * j, b * 128 + 32 * j + 32)
                        self.mm(po[:, cs], vsl, self.hbuf[:, hh, 0, 32 * j:32 * j + 32], True, False,
                                [("vh", b), ("hb", hh, 0)], pok)
                        self.mm(po[:, cs], stb[:, j, :], qt_[:, cs], False, True, kstb + kq0, pok)
                    for j in range(4):
                        bu = 6 + j % 2
                        pu = self.bkf(bu)[:, 0:128]
                        self.mm(pu, self.hbuf[:, hh, 1 + j, :], vsl, True, True, [("hb", hh, 1 + j), ("vh", b)], self.bkk(bu))
                        self.stt(stf[:, j, :], stf[:, j, :], self.decs[:, hh, j:j + 1], pu, ALU.mult, ALU.add,
                                 self.bkk(bu) + [("decs", hh)] + kstf, kstf)
                    self.dma("pool", self.sts[:, hh, :, :].rearrange("s c d -> c s d"), stf, kstf, [], ("sts", hh % 2))
            else:
                for j in range(2):
                    for hh in range(4):
                        qt_ = self.pb(hh, 0)
                        kq0 = self.hk(hh, 0)
                        po = self.bkf(2 + hh)
                        pok = self.bkk(2 + hh)
                        vsl = self.vh[:, b, hh * 128:(hh + 1) * 128]
                        ci = 2 * b + j
                        cs = slice(b * 128 + 64 * j, b * 128 + 64 * j + 64)
                        par = ci % 2
                        noint = gb == 0 and j == 0
                        self.mm(po[:, cs], vsl, self.hbuf[:, hh, 0, 64 * j:64 * j + 64], True, noint,
                                [("vh", b), ("hb", hh, 0)], pok)
                        if not noint:
                            self.mm(po[:, cs], self.Sbf[:, hh, 1 - par, :], qt_[:, cs], False, True,
                                    [("Sbf", hh, 1 - par)] + kq0, pok)
                        bu = 6
                        pu = self.bkf(bu)[:, (hh % 2) * 128:(hh % 2) * 128 + 128]
                        self.mm(pu, self.hbuf[:, hh, 1 + j, :], vsl, True, True, [("hb", hh, 1 + j), ("vh", b)], self.bkk(bu))
                        self.stt(self.Sst[:, hh, :], self.Sst[:, hh, :], self.dec[:, hh, ci:ci + 1], pu, ALU.mult, ALU.add,
                                 self.bkk(bu) + [("dec", hh), ("S", hh)], [("S", hh)])
                        self.cp("act", self.Sbf[:, hh, par, :], self.Sst[:, hh, :], [("S", hh)], [("Sbf", hh, par)])
                    self.qkv_unit(T, "k" if j == 0 else "v", b)
                if gb == self.npb - 1:
                    for hh in range(4):
                        self.dma("pool", self.stp[hh], self.Sst[:, hh, :], [("S", hh)], [], ("stp", hh))
        self.qkv_flush()
        for hh in range(4):
            po = self.bkf(2 + hh)[:, 0:n]
            pok = self.bkk(2 + hh)
            sq = self.pb(9, hh % 2)[:, 0:n]
            sqk = self.hk(9, hh % 2)
            self.act(sq, po, AF.Square, pok, sqk)
            bn = hh % 2
            self.mm(self.bkf(bn)[:, 0:n], self.onesfu[:, :], sq, True, True, sqk + ["onesfu"], self.bkk(bn))
            ms = self.pf(10)[:, 0:n]
            self.act(ms, self.bkf(bn)[:, 0:n], AF.Ln, self.bkk(bn) + ["epsc"], self.pgw(10), scale=1.0 / 128, bias=self.epsc[:, 0:1])
            self.act(ms, ms, AF.Exp, self.pgw(10), self.pgw(10), scale=-0.5)
            self.stt(ms, po, self.cols[:, 32:33], ms, ALU.mult, ALU.mult, pok + ["cols"] + self.pgw(10), self.pgw(10))
            self.tt("dve", self.hT[:, 4 + hh, 0:n], ms, self.gT[:, hh, 0:n], ALU.mult, self.pgw(10) + [("gT", hh)],
                    [("hT", b) for b in range(nb)])

    def run_pipe(self, items, s1, s2, s3, s0=None):
        L = len(items)
        for step in range(L + 2):
            if 2 <= step and s0 is not None:
                s0(items[step - 2], step - 2)
            if step < L:
                s1(items[step], step)
            if 1 <= step <= L:
                s2(items[step - 1], step - 1)
            if 2 <= step:
                s3(items[step - 2], step - 2)

    def attention(self, T):
        nb, n, np_, gb0 = T["nb"], T["n"], T["np"], T["gb0"]
        npc = np_ * 128
        items = []
        for pr in range(4):
            kbs = list(range(gb0 + np_ - 1, -1, -1))
            for i, kb in enumerate(kbs):
                for hl in range(2):
                    items.append(dict(pr=pr, hl=hl, kb=kb, first=(i == 0), last=(i == len(kbs) - 1),
                                      c0=max(0, kb - gb0) * 128, diag=(kb >= gb0)))
        ZB, TB, CB, OB, NB_ = (0, 1), (2, 3), 4, (5, 6), 7
        qk = [("qT", b) for b in range(np_)]

        def s1(w, i):
            s = i % 3
            pr, hl, kb, c0 = w["pr"], w["hl"], w["kb"], w["c0"]
            zb = ZB[i % 2]
            z = self.bkf(zb)[:, c0:npc]
            self.mm(z, self.KT[:, pr, kb * 128:(kb + 1) * 128], self.qz[:, hl, pr, c0:npc], True, True,
                    [("KT", kb)] + qk, self.bkk(zb))
            e = self.pb(3 * s + 1, 0)[:, c0:npc]
            ek = self.hk(3 * s + 1, 0)
            self.act(e, z, AF.Exp, self.bkk(zb), ek, scale=SB_SCALE)
            if w["diag"]:
                ed = self.pb(3 * s + 1, 0)[:, c0:c0 + 128]
                self.tt("pool", ed, ed, self.cmaskA[:, :], ALU.mult, ek + ["cmaskA"], ek)
            sp = self.pb(3 * s + 2, 0)[:, c0:npc]
            self.act(sp, e, AF.Ln, ek, self.hk(3 * s + 2, 0), bias=1.0)

        def s2(w, i):
            s = i % 3
            hl, c0 = w["hl"], w["c0"]
            tb = TB[i % 2]
            Tb = self.bkf(tb)[:, c0:npc]
            sp = self.pb(3 * s + 2, 0)[:, c0:npc]
            spk = self.hk(3 * s + 2, 0)
            Rb = self.pf(9 + hl)
            rk = self.pgw(9 + hl)
            self.mm(Tb, self.triN[:, :], sp, True, True, ["triN"] + spk, self.bkk(tb))
            ex = self.pb(3 * s + 1, 1)[:, c0:npc]
            exk = self.hk(3 * s + 1, 1)
            if w["first"]:
                self.act(ex, Tb, AF.Exp, self.bkk(tb), exk)
            else:
                arg = self.pf(3 * s)[:, c0:npc]
                self.tt("dve", arg, Tb, Rb[:, c0:npc], ALU.add, self.bkk(tb) + rk, self.pgw(3 * s))
                self.act(ex, arg, AF.Exp, self.pgw(3 * s), exk)
            if not w["last"]:
                Cb = self.bkf(CB)[:, c0:npc]
                self.mm(Cb, self.onesN[:, :], sp, True, True, ["onesN"] + spk, self.bkk(CB))
                if w["first"]:
                    if c0 > 0:
                        self.P.op("dve", lambda e: e.memset(Rb[:, 0:c0], 0.0), writes=rk)
                    self.cp("dve", Rb[:, c0:npc], Cb, self.bkk(CB), rk)
                else:
                    self.tt("dve", Rb[:, c0:npc], Cb, Rb[:, c0:npc], ALU.add, self.bkk(CB) + rk, rk)

        def s3(w, i):
            s = i % 3
            pr, hl, kb, c0 = w["pr"], w["hl"], w["kb"], w["c0"]
            ob = OB[hl]
            O = self.bkf(ob)
            wT = self.pb(3 * s + 2, 1)[:, c0:npc]
            if w["first"]:
                self.mm(O[:, 0:npc], self.zeros[:, 0:128], self.zeros[:, 0:npc], True, False, ["zeros"], self.bkk(ob))
            self.mm(O[:, c0:npc], self.V[:, kb, pr * 128:(pr + 1) * 128], wT, False, w["last"],
                    [("V", kb)] + self.hk(3 * s + 2, 1), self.bkk(ob))
            if w["last"] and hl == 1:
                self.attn_norm2(npc, self.hT[:, pr, 0:npc], [("hT", b) for b in range(np_)], OB, NB_, s)

        def s0(w, i):
            s = i % 3
            c0 = w["c0"]
            self.tt("dve", self.pb(3 * s + 2, 1)[:, c0:npc], self.pb(3 * s + 1, 0)[:, c0:npc], self.pb(3 * s + 1, 1)[:, c0:npc],
                    ALU.mult, self.pgw(3 * s + 1), self.hk(3 * s + 2, 1))

        if np_ > 0:
            self.run_pipe(items, s1, s2, s3, s0)
        if T["samp"]:
            self.sample_attention(T)

    def attn_norm2(self, ncols, out, outk, OB, nbk, s):
        sq = self.pb(3 * s + 1, 0)[:, 0:ncols]
        sqk = self.hk(3 * s + 1, 0)
        for hl in range(2):
            ps_ = slice(64 * hl, 64 * hl + 64)
            self.act(sq[ps_, :], self.bkf(OB[hl])[ps_, 0:ncols], AF.Square, self.bkk(OB[hl]), sqk)
        self.mm(self.bkf(nbk)[:, 0:ncols], self.onesbd[:, :], sq, True, True, sqk + ["onesbd"], self.bkk(nbk))
        ms = self.pf(3 * s)[:, 0:ncols]
        msk = self.pgw(3 * s)
        self.act(ms, self.bkf(nbk)[:, 0:ncols], AF.Ln, self.bkk(nbk) + ["epsc"], msk, scale=1.0 / 64, bias=self.epsc[:, 0:1])
        self.act(ms, ms, AF.Exp, msk, msk, scale=-0.5)
        for hl in range(2):
            ps_ = slice(64 * hl, 64 * hl + 64)
            self.stt(out[ps_, :], self.bkf(OB[hl])[ps_, 0:ncols], self.cols[ps_, 33:34], ms[ps_, :], ALU.mult, ALU.mult,
                     self.bkk(OB[hl]) + ["cols"] + msk, outk)

    def attn_norm(self, O, ob, out, outk, ncols, nbk, s, out3=None):
        assert ncols <= 256
        sq = self.pb(3 * s + 1, 0)[:, 0:ncols]
        sqk = self.hk(3 * s + 1, 0)
        self.act(sq, O, AF.Square, self.bkk(ob), sqk)
        self.mm(self.bkf(nbk)[:, 0:ncols], self.onesbd[:, :], sq, True, True, sqk + ["onesbd"], self.bkk(nbk))
        ms = self.pf(3 * s)[:, 0:ncols]
        msk = self.hk(3 * s, 0)
        self.act(ms, self.bkf(nbk)[:, 0:ncols], AF.Ln, self.bkk(nbk) + ["epsc"], msk, scale=1.0 / 64, bias=self.epsc[:, 0:1])
        self.act(ms, ms, AF.Exp, msk, msk, scale=-0.5)
        if out3 is None:
            self.stt(out, O, self.cols[:, 33:34], ms, ALU.mult, ALU.mult, self.bkk(ob) + ["cols"] + msk, outk)
        else:
            self.stt(ms, O, self.cols[:, 33:34], ms, ALU.mult, ALU.mult, self.bkk(ob) + ["cols"] + msk, msk)
            self.cp("dve", out3, ms.rearrange("p (r t) -> p r t", r=4), msk, outk)

    def sample_attention(self, T):
        nb, np_ = T["nb"], T["np"]
        npc = np_ * 128
        PB = self.pastb
        ZB, TB, CB, OB, NB_ = (0, 1), (2, 3), 4, (5, 6), 7
        qkey = [("qT", nb - 1)]
        STG = [(self.pb(0, 1), self.hk(0, 1)), (self.pb(3, 1), self.hk(3, 1)), (self.pb(6, 1), self.hk(6, 1)),
               (self.pb(10, 0), self.hk(10, 0)), (self.pb(10, 1), self.hk(10, 1)), (self.pb(9, 0), self.hk(9, 0))]
        self.ldn = 0
        fifo = []
        LAG = 4

        def finish_oldest():
            i, g = fifo.pop(0)
            st, stk = STG[i % 6]
            bk = NB_ if i % 2 == 0 else CB
            for pr in range(4):
                self.tr(self.bkb(bk)[:, pr * 128:(pr + 1) * 128], st[:, pr * 128:(pr + 1) * 128], self.identb[:],
                        stk + ["identb"], self.bkk(bk))
            tin = self.bkb(bk)[:, 0:512].rearrange("p (r c) -> p r c", r=4)
            self.cp("act" if i % 2 == 0 else "dve", self.KT[:, :, g * 128:(g + 1) * 128], tin, self.bkk(bk), [("KT", g)])

        def load_blocks(j, g):
            i = self.ldn
            self.ldn += 1
            self.dma("pool", self.V[:, g, :], self.cv[j, g * 128:(g + 1) * 128, :], [], [("V", g)], ("ldcv", i % 4))
            st, stk = STG[i % 6]
            self.dma("pool", st, self.ck[j, g * 128:(g + 1) * 128, :], [], stk, ("ldck", i % 6))
            fifo.append((i, g))
            if len(fifo) > LAG:
                finish_oldest()

        for hl in range(2):
            ps_ = slice(64 * hl, 64 * hl + 64)
            self.cp("dve", self.qbd[ps_, :, :, 32 * hl:32 * hl + 32],
                    self.qz[ps_, hl, :, npc:npc + 128].rearrange("p r (j t) -> p r j t", j=4), qkey, ["qbd"])

        def kv(w):
            if w["kb"] == "new":
                return (lambda pr: self.kTs[:, pr, :]), ["kTs"], (lambda h: self.vsb[:, h * 64:(h + 1) * 64]), ["vsb"]
            kb = w["kb"]
            return ((lambda pr: self.KT[:, pr, kb * 128:(kb + 1) * 128]), [("KT", kb)],
                    (lambda h: self.V[:, kb, h * 64:(h + 1) * 64]), [("V", kb)])

        def s1(w, i):
            s = i % 3
            j = w["j"]
            kf, kk, vf, vk = kv(w)
            zb = ZB[i % 2]
            z = self.bkf(zb)[:, 0:256]
            for pr in range(4):
                self.mm(z[:, pr * 64:(pr + 1) * 64], kf(pr), self.qbd[:, pr, j, :], True, True,
                        kk + ["qbd"], self.bkk(zb))
            e = self.pb(3 * s + 1, 0)[:, 0:256]
            ek = self.hk(3 * s + 1, 0)
            self.act(e, z, AF.Exp, self.bkk(zb), ek, scale=SB_SCALE)
            if w["kb"] == "new":
                self.tt("pool", e, e, self.smask[:, j, :], ALU.mult, ek + ["smask"], ek)
            sp = self.pb(3 * s + 2, 0)[:, 0:256]
            self.act(sp, e, AF.Ln, ek, self.hk(3 * s + 2, 0), bias=1.0)

        def s2(w, i):
            s = i % 3
            tb = TB[i % 2]
            Tb = self.bkf(tb)[:, 0:256]
            sp = self.pb(3 * s + 2, 0)[:, 0:256]
            spk = self.hk(3 * s + 2, 0)
            Rb = self.pf(9)[:, 256:512]
            rk = self.hk(9, 1)
            self.mm(Tb, self.triN[:, :], sp, True, True, ["triN"] + spk, self.bkk(tb))
            ex = self.pb(3 * s + 1, 1)[:, 0:256]
            exk = self.hk(3 * s + 1, 1)
            if w["first"]:
                self.act(ex, Tb, AF.Exp, self.bkk(tb), exk)
            else:
                arg = self.pf(3 * s)[:, 0:256]
                self.tt("dve", arg, Tb, Rb, ALU.add, self.bkk(tb) + rk, self.hk(3 * s, 0))
                self.act(ex, arg, AF.Exp, self.hk(3 * s, 0), exk)
            if not w["last"]:
                Cb = self.bkf(tb)[:, 256:512]
                self.mm(Cb, self.onesN[:, :], sp, True, True, ["onesN"] + spk, self.bkk(tb))
                if w["first"]:
                    self.cp("dve", Rb, Cb, self.bkk(tb), rk)
                else:
                    self.tt("dve", Rb, Cb, Rb, ALU.add, self.bkk(tb) + rk, rk)

        def s0(w, i):
            s = i % 3
            self.tt("dve", self.pb(3 * s + 2, 1)[:, 0:256], self.pb(3 * s + 1, 0)[:, 0:256], self.pb(3 * s + 1, 1)[:, 0:256],
                    ALU.mult, self.pgw(3 * s + 1), self.hk(3 * s + 2, 1))

        def s3(w, i):
            s = i % 3
            j = w["j"]
            kf, kk, vf, vk = kv(w)
            ob = OB[j % 2]
            O = self.bkf(ob)
            if w["first"]:
                self.mm(O[:, 0:128], self.zeros[:, 0:128], self.zeros[:, 0:128], True, False, ["zeros"], self.bkk(ob))
            wT = self.pb(3 * s + 2, 1)
            for h in range(8):
                pr, hl = h // 2, h % 2
                self.mm(O[64 * hl:64 * hl + 64, pr * 32:(pr + 1) * 32], vf(h), wT[:, h * 32:(h + 1) * 32], False,
                        w["last"] and h >= 6, vk + self.hk(3 * s + 2, 1), self.bkk(ob))
            if w["last"]:
                q0 = npc + 32 * j
                self.attn_norm(O[:, 0:128], ob, None, [("hT", nb - 1)], 128, NB_, s, out3=self.hT[:, 0:4, q0:q0 + 32])
            if w["kb"] != "new" and j < 3:
                load_blocks(j + 1, w["kb"])
                if j + 1 == 3 and w["kb"] == 0:
                    while fifo:
                        finish_oldest()

        assert PB >= 8, "interleaved cache reload needs the next sequence's first reader well behind the staged transposes"
        for g0 in range(PB - 1, -1, -1):
            load_blocks(0, g0)
        while fifo:
            finish_oldest()
        items = []
        for j in range(4):
            kbs = ["new"] + list(range(PB - 1, -1, -1))
            items += [dict(j=j, kb=kb, first=(i == 0), last=(i == len(kbs) - 1)) for i, kb in enumerate(kbs)]
        self.run_pipe(items, s1, s2, s3, s0)
        assert not fifo

    def outproj(self, T):
        ti, nb = T["ti"], T["nb"]
        rs = [self.use(ti, ("out", 0)), self.use(ti, ("out", 1), oldest=("out", 0))]
        for b in range(nb):
            for nh in range(2):
                bk = self.nextbank()
                for k in range(8):
                    r = rs[k // 4]
                    w = self.wview(r, "rows")
                    self.mm(self.bkf(bk)[:, :], self.hT[:, k, b * 128:(b + 1) * 128], w[:, k % 4, nh * 512:(nh + 1) * 512],
                            k == 0, k == 7, [("w", r), ("hT", b)], self.bkk(bk))
                xs_ = self.xt[:, b, nh * 512:(nh + 1) * 512]
                self.tt("dve", xs_, self.bkf(bk)[:, :], xs_, ALU.add, self.bkk(bk) + [("xt", b)], [("xt", b)])

    def tile(self, ti):
        gb0, np_, samp = self.tiles[ti]
        nb = np_ + (1 if samp else 0)
        T = dict(ti=ti, gb0=gb0, np=np_, samp=samp, nb=nb, n=nb * 128)
        for b in range(nb):
            if samp and b == nb - 1:
                src = self.xs
            else:
                src = self.xp[(gb0 + b) * 128:(gb0 + b + 1) * 128, :]
            self.dma("sp", self.xt[:, b, :], src, [], [("xt", b)], ("ldx", b))
        self.ffn(T, 1)
        self.inproj(T)
        self.hgrn(T)
        self.attention(T)
        self.outproj(T)
        self.ffn(T, 2)
        for b in range(nb):
            if samp and b == nb - 1:
                dst = self.ys
            else:
                dst = self.yp[(gb0 + b) * 128:(gb0 + b + 1) * 128, :]
            self.dma("pool", dst, self.xt[:, b, :], [("xt", b)], [], ("sty", b))

    def build(self):
        nc = self.nc
        with contextlib.ExitStack() as es:
            self.declare(es)
            self.setup()
            for ti in range(len(self.tiles)):
                self.tile(ti)
            esem = {e: es.enter_context(nc.semaphore("s_" + e)) for e in ENGS}
            dsem = {}
            for i, k in enumerate(self.P.dcount.keys()):
                dsem[k] = es.enter_context(nc.semaphore("d%d" % i))
            with nc.Block() as block:
                self.P.emit(block, esem, dsem)
        return nc


_CACHE = {}


def _get_nc(npb, pastb):
    key = (npb, pastb)
    if key not in _CACHE:
        _CACHE[key] = Builder(npb, pastb).build()
    return _CACHE[key]


def kernel(x_prompt, x_sample, cache_sb_k, cache_sb_v, state_hgrn,
           ffn1_norm, ffn1_w_in, ffn1_w_out, mix_norm, w_in, sb_q_gain, sb_k_gain,
           hg_lb_logits, sb_out_gain, hg_out_gain, w_out, ffn2_norm, ffn2_w_in, ffn2_w_out):
    f = lambda a: np.ascontiguousarray(np.asarray(a, dtype=np.float32))
    x_prompt, x_sample = f(x_prompt), f(x_sample)
    cache_sb_k, cache_sb_v, state_hgrn = f(cache_sb_k), f(cache_sb_v), f(state_hgrn)
    B, S, _ = x_prompt.shape
    DB, DS, _ = x_sample.shape
    PAST = cache_sb_k.shape[2]
    assert B == 8 and DB == 32 and DS == 32 and S % 128 == 0 and PAST % 128 == 0
    npb, pastb = S // 128, PAST // 128
    nc = _get_nc(npb, pastb)
    vrows = np.concatenate([
        f(ffn1_norm)[0].reshape(8, 128), f(mix_norm)[0].reshape(8, 128), f(ffn2_norm)[0].reshape(8, 128),
        f(hg_lb_logits)[0].reshape(4, 128), f(hg_lb_logits)[1].reshape(4, 128),
        f(hg_out_gain)[0].reshape(1, 128), np.tile(f(sb_out_gain)[0], 2).reshape(1, 128)], axis=0)
    qkg = np.stack([f(sb_q_gain)[0], f(sb_k_gain)[0]], axis=0)
    shared = {"f1wi": f(ffn1_w_in)[0], "f1wo": f(ffn1_w_out)[0], "win": f(w_in)[0], "wout": f(w_out)[0],
              "f2wi": f(ffn2_w_in)[0], "f2wo": f(ffn2_w_out)[0], "vrows": np.ascontiguousarray(vrows),
              "qkg": np.ascontiguousarray(qkg)}
    in_maps = []
    for c in range(8):
        m = dict(shared)
        m["xp"] = x_prompt[c]
        m["xs"] = np.ascontiguousarray(x_sample[4 * c:4 * c + 4].reshape(128, D))
        m["ck"] = np.ascontiguousarray(cache_sb_k[0, 4 * c:4 * c + 4].reshape(4, PAST, 512))
        m["cv"] = np.ascontiguousarray(cache_sb_v[0, 4 * c:4 * c + 4].reshape(4, PAST, 512))
        m["s0"] = np.ascontiguousarray(state_hgrn[0, 4 * c:4 * c + 4])
        in_maps.append(m)
    res = run_bass_kernel_spmd(nc, in_maps, core_ids=list(range(8)))
    R = res.results
    y_prompt = np.stack([R[c]["yp"] for c in range(8)], axis=0)
    y_sample = np.concatenate([R[c]["ys"].reshape(4, 32, D) for c in range(8)], axis=0)
    kp = np.stack([R[c]["kp"].reshape(S, 8, 64) for c in range(8)], axis=0)[None]
    vp = np.stack([R[c]["vp"].reshape(S, 8, 64) for c in range(8)], axis=0)[None]
    stp = np.stack([R[c]["stp"] for c in range(8)], axis=0)[None]
    ks = np.concatenate([R[c]["ks"].reshape(4, 32, 8, 64) for c in range(8)], axis=0)[None]
    vs = np.concatenate([R[c]["vs"].reshape(4, 32, 8, 64) for c in range(8)], axis=0)[None]
    sts = np.concatenate([R[c]["sts"] for c in range(8)], axis=0)[None]
    outs = (y_prompt, y_sample, kp, vp, stp, ks, vs, sts)
    return tuple(np.ascontiguousarray(o, dtype=np.float32) for o in outs)
```

```python
import contextlib
import numpy as np
import concourse.bass as bass
import concourse.mybir as mybir
from concourse.bass_utils import run_bass_kernel_spmd

F32 = mybir.dt.float32
BF16 = mybir.dt.bfloat16
AF = mybir.ActivationFunctionType
ALU = mybir.AluOpType
AX = mybir.AxisListType

D = 1024
KC = 8
DFF = 2816
NFC = 22
EPS = 1e-6
SB_SCALE = 0.125
NR = 5
NPG = 11

ENGS = ("pe", "act", "dve", "pool", "sp")
SAME_ENGINE_SYNC = {"pe": False, "act": True, "dve": True, "pool": True, "sp": False}


class Op:
    __slots__ = ("eng", "fn", "deps", "needs_inc", "semval", "dkey", "dval")


class Prog:
    def __init__(self):
        self.ops = {e: [] for e in ENGS}
        self.lw = {}
        self.rd = {}
        self.dcount = {}
        self.dlast = {}

    def op(self, eng, fn, reads=(), writes=(), dma=None):
        o = Op()
        o.eng, o.fn, o.needs_inc, o.semval = eng, fn, False, None
        o.dkey, o.dval = None, None
        writes = list(writes) + [r for r in reads if isinstance(r, tuple) and r[0] == "ps" and r not in writes]
        deps = []
        for r in reads:
            t = self.lw.get(r)
            if t is not None:
                deps.append(t)
        for w in writes:
            t = self.lw.get(w)
            if t is not None:
                deps.append(t)
            for t in self.rd.get(w, {}).values():
                deps.append(t)
        if dma is not None:
            t = self.dlast.get(dma)
            if t is not None:
                deps.append(t)
            self.dcount[dma] = self.dcount.get(dma, 0) + 16
            o.dkey, o.dval = dma, self.dcount[dma]
            self.dlast[dma] = o
        o.deps = deps
        for t in deps:
            if t.dkey is None:
                t.needs_inc = True
        for w in writes:
            self.lw[w] = o
            self.rd[w] = {}
        for r in reads:
            if r in writes:
                continue
            self.rd.setdefault(r, {})[dma if dma is not None else eng] = o
        self.ops[eng].append(o)
        return o

    def emit(self, block, esem, dsem):
        for e in ENGS:
            c = 0
            for o in self.ops[e]:
                if o.dkey is None and o.needs_inc:
                    c += 1
                    o.semval = c
        prog = self

        def mk(e):
            def body(engine):
                seen = {}
                for o in prog.ops[e]:
                    for t in o.deps:
                        if t.dkey is not None:
                            sem, val, key = dsem[t.dkey], t.dval, ("d", t.dkey)
                        else:
                            if t.eng == e and not SAME_ENGINE_SYNC[e]:
                                continue
                            sem, val, key = esem[t.eng], t.semval, ("e", t.eng)
                        if seen.get(key, 0) < val:
                            engine.wait_ge(sem, val)
                            seen[key] = val
                    inst = o.fn(engine)
                    if o.dkey is not None:
                        inst.then_inc(dsem[o.dkey], 16)
                    elif o.needs_inc:
                        inst.then_inc(esem[e], 1)
                if e == "sp":
                    for k, v in prog.dcount.items():
                        if seen.get(("d", k), 0) < v:
                            engine.wait_ge(dsem[k], v)
            return body

        block.tensor(mk("pe"))
        block.scalar(mk("act"))
        block.vector(mk("dve"))
        block.gpsimd(mk("pool"))
        block.sync(mk("sp"))


def plan_tiles(npb):
    tot = npb + 1
    nt = -(-tot // 4)
    base, extra = divmod(tot, nt)
    sizes = [base + 1] * extra + [base] * (nt - extra)
    tiles = []
    g = 0
    for i, s in enumerate(sizes):
        last = i == nt - 1
        n_p = s - 1 if last else s
        tiles.append((g, n_p, last))
        g += n_p
    assert g == npb
    return tiles


SLOTS = ([("f1u", g) for g in range(11)] + [("f1d", g) for g in range(6)] +
         [("in", c) for c in (0, 1, 2, 5, 3, 4, 6)] + [("out", g) for g in range(2)] +
         [("f2u", g) for g in range(11)] + [("f2d", g) for g in range(6)])
SLOTPOS = {s: i for i, s in enumerate(SLOTS)}
NSLOT = len(SLOTS)


class Builder:
    def __init__(self, npb, pastb):
        self.npb, self.pastb = npb, pastb
        self.tiles = plan_tiles(npb)
        self.nc = bass.Bass("TRN2", target_bir_lowering=False)
        self.P = Prog()
        self.bank_rr = 0
        self.issued = 0

    def declare(self, es):
        nc = self.nc
        S, PAST = self.npb * 128, self.pastb * 128

        def din(name, shape):
            return nc.dram_tensor(name, shape, F32, kind="ExternalInput").ap()

        def dout(name, shape):
            return nc.dram_tensor(name, shape, F32, kind="ExternalOutput").ap()

        self.xp = din("xp", [S, D])
        self.xs = din("xs", [128, D])
        self.ck = din("ck", [4, PAST, 512])
        self.cv = din("cv", [4, PAST, 512])
        self.s0 = din("s0", [4, 4, 128, 128])
        self.wm = {"f1u": din("f1wi", [D, 2 * DFF]), "f1d": din("f1wo", [DFF, D]),
                   "in": din("win", [D, 3584]), "out": din("wout", [D, D]),
                   "f2u": din("f2wi", [D, 2 * DFF]), "f2d": din("f2wo", [DFF, D])}
        self.vrows = din("vrows", [34, 128])
        self.qkg = din("qkg", [2, 64])
        self.yp = dout("yp", [S, D])
        self.ys = dout("ys", [128, D])
        self.kp = dout("kp", [S, 512])
        self.vp = dout("vp", [S, 512])
        self.stp = dout("stp", [4, 128, 128])
        self.ks = dout("ks", [128, 512])
        self.vs = dout("vs", [128, 512])
        self.sts = dout("sts", [4, 4, 128, 128])
        self.wsc = nc.dram_tensor("wsc", [NSLOT, 128, 4096], BF16, kind="Internal").ap()

        def sb(name, shape, dt):
            return es.enter_context(nc.sbuf_tensor(name, shape, dt))

        self.xt = sb("xt", [128, 4, D], F32)
        self.hT = sb("hT", [128, 8, 512], BF16)
        self.KT = sb("KT", [128, 4, max(self.npb, self.pastb) * 128], BF16)
        self.V = sb("V", [128, max(self.npb, self.pastb), 512], BF16)
        self.wr = sb("wr", [128, NR, 4096], BF16)
        self.pgb = sb("pg", [128, NPG, 1024], BF16)
        self.sgT = sb("sgT", [128, 4, 512], F32)
        self.qhT = sb("qhT", [128, 4, 512], BF16)
        self.gT = sb("gT", [128, 4, 512], BF16)
        self.qz = sb("qz", [128, 2, 4, 512], BF16)
        self.vh = sb("vh", [128, 4, 512], BF16)
        self.hbuf = sb("hbuf", [128, 4, 5, 128], BF16)
        self.kTs = sb("kTs", [128, 4, 128], BF16)
        self.vsb = sb("vsb", [128, 512], BF16)
        self.qbd = sb("qbd", [128, 4, 4, 64], BF16)
        self.identb = sb("identb", [128, 128], BF16)
        self.identf = sb("identf", [128, 128], F32)
        self.onesf = sb("onesf", [128, 256], F32)
        self.triN = sb("triN", [128, 128], BF16)
        self.othN = sb("othN", [128, 128], BF16)
        self.onesN = sb("onesN", [128, 128], BF16)
        self.cmaskA = sb("cmaskA", [128, 128], F32)
        self.hmask64 = sb("hmask64", [128, 128], F32)
        self.hmask32 = sb("hmask32", [128, 128], F32)
        self.onesbd = sb("onesbd", [128, 128], BF16)
        self.onesfu = sb("onesfu", [128, 128], BF16)
        self.zeros = sb("zeros", [128, 512], BF16)
        self.smask = sb("smask", [128, 4, 256], F32)
        self.rmp = sb("rmp", [128, 512], F32)
        self.rms = sb("rms", [128, 128], F32)
        self.rm64 = sb("rm64", [128, 2], F32)
        self.rm32 = sb("rm32", [128, 4], F32)
        self.gq = sb("gq", [128, 64], F32)
        self.gk = sb("gk", [128, 64], F32)
        self.vr = sb("vr", [34, 128], F32)
        self.cols = sb("cols", [128, 40], F32)
        self.negh = sb("negh", [128, 1], F32)
        self.epsc = sb("epsc", [128, 1], F32)
        self.Sst = sb("Sst", [128, 4, 128], F32)
        self.Sbf = sb("Sbf", [128, 4, 2, 128], BF16)
        self.dec = sb("dec", [128, 4, 8], F32)
        self.decs = sb("decs", [128, 4, 4], F32)
        self.ssq = sb("ssq", [128, 8], F32)
        self.rstd = sb("rstd", [128, 8], F32)
        self.ss8 = sb("ss8", [128, 2, 8], F32)
        self.rs8 = sb("rs8", [128, 2, 8], F32)
        self.banks = [es.enter_context(nc.psum_tensor("bk%d" % i, [128, 512], F32)) for i in range(8)]

    def pf(self, i):
        return self.pgb[:, i, :].bitcast(F32)

    def pb(self, i, h):
        return self.pgb[:, i, h * 512:(h + 1) * 512]

    def bkk(self, bk):
        return [("ps", bk)]

    def bks(self, bk, *subs):
        return [("ps", bk)]

    def hk(self, i, h):
        return [("act", 2 * i + h)]

    def pgw(self, i):
        return [("act", 2 * i), ("act", 2 * i + 1)]

    def bkf(self, i):
        return self.banks[i][:, :]

    def bkb(self, i):
        return self.banks[i][:, :].bitcast(BF16)

    def nextbank(self):
        b = self.bank_rr
        self.bank_rr = (self.bank_rr + 1) % 8
        return b

    def mm(self, out, lhsT, rhs, start, stop, reads, writes):
        self.P.op("pe", lambda e: e.matmul(out, lhsT=lhsT, rhs=rhs, start=start, stop=stop),
                  reads=reads, writes=writes)

    def tr(self, out, in_, ident, reads, writes):
        self.P.op("pe", lambda e: e.transpose(out=out, in_=in_, identity=ident), reads=reads, writes=writes)

    def act(self, out, in_, func, reads, writes, **kw):
        self.P.op("act", lambda e: e.activation(out=out, in_=in_, func=func, **kw), reads=reads, writes=writes)

    def ts(self, eng, out, in0, s1, s2, op0, op1, reads, writes):
        if op1 is None:
            self.P.op(eng, lambda e: e.tensor_scalar(out=out, in0=in0, scalar1=s1, scalar2=None, op0=op0),
                      reads=reads, writes=writes)
        else:
            self.P.op(eng, lambda e: e.tensor_scalar(out=out, in0=in0, scalar1=s1, scalar2=s2, op0=op0, op1=op1),
                      reads=reads, writes=writes)

    def tt(self, eng, out, in0, in1, op, reads, writes):
        self.P.op(eng, lambda e: e.tensor_tensor(out=out, in0=in0, in1=in1, op=op), reads=reads, writes=writes)

    def stt(self, out, in0, scalar, in1, op0, op1, reads, writes):
        self.P.op("dve", lambda e: e.scalar_tensor_tensor(out=out, in0=in0, scalar=scalar, in1=in1, op0=op0, op1=op1),
                  reads=reads, writes=writes)

    def cp(self, eng, out, in_, reads, writes):
        if eng == "act":
            self.act(out, in_, AF.Copy, reads, writes)
        else:
            self.P.op(eng, lambda e: e.tensor_copy(out=out, in_=in_), reads=reads, writes=writes)

    def dma(self, eng, out, in_, reads, writes, key):
        self.P.op(eng, lambda e: e.dma_start(out=out, in_=in_), reads=reads, writes=writes, dma=key)

    def pow_rstd(self, out, in_, n, reads, writes):
        nh = self.negh[:, 0:1] if n == 1 else self.negh[:, 0:1].to_broadcast([128, n])
        self.tt("pool", out, in_, nh, ALU.pow, list(reads) + ["negh"], writes)

    def setup(self):
        P = self.P

        def pool(fn, reads=(), writes=()):
            P.op("pool", fn, reads=reads, writes=writes)

        def sel(out, in_, pattern, cmp, base, cm, reads, writes):
            pool(lambda e: e.affine_select(out=out, in_=in_, pattern=pattern, compare_op=cmp, fill=0.0,
                                           base=base, channel_multiplier=cm), reads, writes)

        pool(lambda e: e.memset(self.onesf[:], 1.0), writes=["onesf"])
        pool(lambda e: e.memset(self.negh[:], -0.5), writes=["negh"])
        pool(lambda e: e.memset(self.epsc[:], EPS), writes=["epsc"])
        pool(lambda e: e.memset(self.zeros[:], 0.0), writes=["zeros"])
        pool(lambda e: e.memset(self.qbd[:], 0.0), writes=["qbd"])
        pool(lambda e: e.memset(self.qz[:], 0.0), writes=[("qT", b) for b in range(4)])
        pool(lambda e: e.memset(self.onesfu[:], 1.0), writes=["onesfu"])
        pool(lambda e: e.memset(self.onesN[:], -1.0), writes=["onesN"])
        pool(lambda e: e.memset(self.Sst[:], 0.0), writes=[("S", h) for h in range(4)])
        of = self.onesf[:, 0:128]
        sel(self.identf[:], of, [[-1, 128]], ALU.is_equal, 0, 1, ["onesf"], ["identf"])
        sel(self.identb[:], of, [[-1, 128]], ALU.is_equal, 0, 1, ["onesf"], ["identb"])
        sel(self.cmaskA[:], of, [[-1, 128]], ALU.is_ge, 0, 1, ["onesf"], ["cmaskA"])
        pool(lambda e: e.tensor_scalar(out=self.triN[:], in0=self.cmaskA[:], scalar1=-1.0, scalar2=None, op0=ALU.mult),
             ["cmaskA"], ["triN"])
        sel(self.hmask64[:], of, [[1, 128]], ALU.is_gt, 0, -1, ["onesf"], ["hmask64"])
        pool(lambda e: e.tensor_scalar(out=self.othN[:], in0=self.hmask64[:], scalar1=-1.0, scalar2=None, op0=ALU.mult),
             ["hmask64"], ["othN"])
        sel(self.cmaskA[:], of, [[1, 128]], ALU.is_gt, 0, -1, ["onesf", "triN"], ["cmaskA"])
        sel(self.hmask64[:], of, [[1, 128]], ALU.is_ge, 0, -1, ["onesf", "othN"], ["hmask64"])
        sel(self.hmask32[:], of, [[1, 128]], ALU.is_ge, 0, -1, ["onesf"], ["hmask32"])
        pool(lambda e: e.memset(self.hmask64[0:64, 64:128], 0.0), ["hmask64"], ["hmask64"])
        for i in range(3):
            pool(lambda e, i=i: e.memset(self.hmask32[32 * i:32 * i + 32, 32 * (i + 1):128], 0.0), ["hmask32"], ["hmask32"])
        pool(lambda e: e.memset(self.onesbd[:], 0.0), writes=["onesbd"])
        pool(lambda e: e.memset(self.onesbd[0:64, 0:64], 1.0), ["onesbd"], ["onesbd"])
        pool(lambda e: e.memset(self.onesbd[64:128, 64:128], 1.0), ["onesbd"], ["onesbd"])
        for C, rm in ((64, self.rm64), (32, self.rm32)):
            for j in range(128 // C):
                sel(rm[:, j:j + 1], self.onesf[:, 0:1], [[0, 1]], ALU.is_ge, -j * C, 1, ["onesf"], ["rm%d" % C])
                sel(rm[:, j:j + 1], rm[:, j:j + 1], [[0, 1]], ALU.is_ge, (j + 1) * C - 1, -1, ["rm%d" % C], ["rm%d" % C])
        for j in range(4):
            sel(self.smask[:, j, :], self.onesf[:, 0:256], [[0, 8], [1, 32]], ALU.is_gt, 32 * j, -1, ["onesf"], ["smask"])
            sel(self.smask[:, j, :], self.smask[:, j, :], [[0, 256]], ALU.is_ge, -32 * j, 1, ["smask"], ["smask"])
        pool(lambda e: e.memset(self.rmp[:], 1.0), writes=["rmp"])
        pool(lambda e: e.memset(self.rmp[:].rearrange("p (a b) -> p a b", b=64)[:, :, 0:1], 0.0), ["rmp"], ["rmp"])
        pool(lambda e: e.memset(self.rms[:], 1.0), writes=["rms"])
        pool(lambda e: e.memset(self.rms[:].rearrange("p (a b) -> p a b", b=32)[:, :, 0:1], 0.0), ["rms"], ["rms"])
        self.dma("sp", self.vr[:], self.vrows, [], ["vr"], "ldv")
        self.dma("sp", self.gq[:], self.qkg[0:1, :].to_broadcast([128, 64]),
                 [], ["gq"], "ldgq")
        self.dma("sp", self.gk[:], self.qkg[1:2, :].to_broadcast([128, 64]), [], ["gk"], "ldgk")
        bk = self.nextbank()
        self.tr(self.bkf(bk)[:, 0:34], self.vr[:], self.identf[0:34, 0:34], ["vr", "identf"], self.bkk(bk))
        self.cp("dve", self.cols[:, 0:34], self.bkf(bk)[:, 0:34], self.bkk(bk), ["cols"])
        c = self.cols
        self.tt("dve", c[:, 34:38], c[:, 28:32], c[:, 24:28], ALU.subtract, ["cols"], ["cols"])
        self.act(c[:, 34:38], c[:, 34:38], AF.Exp, ["cols"], ["cols"])
        self.ts("dve", c[:, 34:38], c[:, 34:38], 1.0, None, ALU.add, None, ["cols"], ["cols"])
        self.P.op("dve", lambda e: e.reciprocal(out=c[:, 34:38], in_=c[:, 34:38]), reads=["cols"], writes=["cols"])
        self.ts("dve", c[:, 24:28], c[:, 34:38], 0.5, 0.5, ALU.mult, ALU.add, ["cols"], ["cols"])
        self.ts("dve", c[:, 28:32], c[:, 34:38], -0.5, 0.5, ALU.mult, ALU.add, ["cols"], ["cols"])
        self.ts("dve", c[:, 34:38], c[:, 28:32], -1.0, None, ALU.mult, None, ["cols"], ["cols"])

    def wview(self, r, kind):
        if kind == "cols":
            return self.wr[:, r, :].rearrange("p (k c) -> p k c", k=8)
        return self.wr[:, r, :].rearrange("p (c n) -> p c n", c=4)

    def issue_load(self, u):
        ti, pos = divmod(u, NSLOT)
        r = u % NR
        name, g = SLOTS[pos]
        if ti == 0:
            W = self.wm[name]
            if name in ("f1u", "f2u", "in"):
                src = W[:, g * 512:(g + 1) * 512].rearrange("(k p) c -> p k c", p=128)
                dst = self.wview(r, "cols")
            else:
                R = W.shape[0]
                n = min(4, R // 128 - 4 * g)
                src = W[g * 512:g * 512 + n * 128, :].rearrange("(c p) n -> p c n", p=128)
                dst = self.wview(r, "rows")[:, 0:n, :]
            self.dma("pool", dst, src, [], [("w", r)], ("wlc", r))
            if len(self.tiles) > 1:
                self.dma("sp", self.wsc[pos], self.wr[:, r, :], [("w", r)], [("wsc", pos)], ("wst", r))
        else:
            self.dma("sp", self.wr[:, r, :], self.wsc[pos], [("wsc", pos)], [("w", r)], ("wl", r))

    def use(self, ti, slot, oldest=None):
        u = ti * NSLOT + SLOTPOS[slot]
        base = u if oldest is None else ti * NSLOT + SLOTPOS[oldest]
        lim = min(base + NR - 1, len(self.tiles) * NSLOT - 1)
        while self.issued <= lim:
            self.issue_load(self.issued)
            self.issued += 1
        assert self.issued > u
        return u % NR

    def rmsnorm(self, T, gcol0):
        nb = T["nb"]
        for b in range(nb):
            xk = ("xt", b)
            self.P.op("dve", lambda e, b=b: e.memset(self.ssq[:, b:b + 1], 0.0), writes=[("ssq", b)])
            self.act(self.pgb[:, b, :], self.xt[:, b, :], AF.Square, [xk, ("ssq", b)], self.pgw(b) + [("ssq", b)],
                     accum_out=self.ssq[:, b:b + 1])
            self.ts("dve", self.rstd[:, b:b + 1], self.ssq[:, b:b + 1], 1.0 / D, EPS, ALU.mult, ALU.add,
                    [("ssq", b)], [("rstd", b)])
            self.pow_rstd(self.rstd[:, b:b + 1], self.rstd[:, b:b + 1], 1, [("rstd", b)], [("rstd", b)])
        for b in range(nb):
            if b % 2 == 0:
                self.ts("dve", self.pgb[:, b, :], self.xt[:, b, :], self.rstd[:, b:b + 1], None,
                        ALU.mult, None, [("xt", b), ("rstd", b)], self.pgw(b))
            else:
                self.act(self.pgb[:, b, :], self.xt[:, b, :], AF.Copy, [("xt", b), ("rstd", b)], self.pgw(b),
                         scale=self.rstd[:, b:b + 1])
        bks = []
        for b in range(nb):
            bk = self.nextbank()
            bks.append(bk)
            xn = self.pgb[:, b, :]
            for k in range(8):
                self.tr(self.bkb(bk)[:, k * 128:(k + 1) * 128], xn[:, k * 128:(k + 1) * 128], self.identb[:],
                        self.pgw(b) + ["identb"], self.bkk(bk))
        for b in range(nb):
            bk = bks[b]
            self.tt("dve", self.hT[:, :, b * 128:(b + 1) * 128],
                    self.bkb(bk)[:, 0:1024].rearrange("p (k c) -> p k c", k=8),
                    self.cols[:, gcol0:gcol0 + 8].unsqueeze(2).to_broadcast([128, 8, 128]), ALU.mult,
                    self.bkk(bk) + ["cols"], [("hT", b)])

    def ffn(self, T, which):
        ti, nb, n = T["ti"], T["nb"], T["n"]
        up, dn = ("f1u", "f1d") if which == 1 else ("f2u", "f2d")
        self.rmsnorm(T, 0 if which == 1 else 16)
        hk = [("hT", b) for b in range(nb)]
        for mc in range(44):
            r = self.use(ti, (up, mc // 4))
            w = self.wview(r, "cols")
            bk = self.nextbank()
            for k in range(8):
                self.mm(self.bkf(bk)[:, 0:n], w[:, k, (mc % 4) * 128:(mc % 4) * 128 + 128], self.hT[:, k, 0:n],
                        k == 0, k == 7, [("w", r)] + hk, self.bkk(bk))
            c = mc % 22
            a = self.pb(c // 2, c % 2)[:, 0:n]
            if mc < 22:
                self.act(a, self.bkf(bk)[:, 0:n], AF.Silu, self.bkk(bk), [("act", c)])
            else:
                self.tt("dve", a, self.bkf(bk)[:, 0:n], a, ALU.mult, self.bkk(bk) + [("act", c)], [("act", c)])
        for grp in range(3):
            rs = {}
            rs[2 * grp] = self.use(ti, (dn, 2 * grp))
            rs[2 * grp + 1] = self.use(ti, (dn, 2 * grp + 1), oldest=(dn, 2 * grp))
            chunks = [c for c in range(8 * grp, min(8 * grp + 8, NFC))]
            for b in range(nb):
                for nh in range(2):
                    bk = self.nextbank()
                    for i, c in enumerate(chunks):
                        r = rs[c // 4]
                        w = self.wview(r, "rows")
                        self.mm(self.bkf(bk)[:, :], self.pb(c // 2, c % 2)[:, b * 128:(b + 1) * 128],
                                w[:, c % 4, nh * 512:(nh + 1) * 512], i == 0, i == len(chunks) - 1,
                                [("w", r), ("act", c)], self.bkk(bk))
                    xs_ = self.xt[:, b, nh * 512:(nh + 1) * 512]
                    self.stt(xs_, self.bkf(bk)[:, :], 0.5, xs_, ALU.mult, ALU.add, self.bkk(bk) + [("xt", b)], [("xt", b)])

    def inproj(self, T):
        ti, nb, n, gb0 = T["ti"], T["nb"], T["n"], T["gb0"]
        self.rmsnorm(T, 8)
        hk = [("hT", b) for b in range(nb)]
        h8 = "p (h d) -> p h d"
        for (cg, kind) in ((0, "q"), (1, "k"), (2, "v"), (5, "hi")):
            r = self.use(ti, ("in", cg))
            w = self.wview(r, "cols")
            pending = None
            for b in range(nb):
                samp = T["samp"] and b == nb - 1
                gb = gb0 + b
                bk = self.nextbank()
                for k in range(8):
                    self.mm(self.bkf(bk)[:, :], self.hT[:, k, b * 128:(b + 1) * 128], w[:, k, :], k == 0, k == 7,
                            [("w", r), ("hT", b)], self.bkk(bk))
                par = b % 2
                pA, pB, pC, pD, pE = (5 * par + i for i in range(5))
                psk = self.bkk(bk)
                if kind in ("q", "k"):
                    sq = self.pf(pA)
                    self.act(sq, self.bkf(bk)[:, :], AF.Square, psk, self.pgw(pA))
                    self.P.op("dve", lambda e, sq=sq, par=par: e.tensor_reduce(
                        out=self.ss8[:, par, :], in_=sq.rearrange(h8, h=8), axis=AX.X, op=ALU.add),
                        reads=self.pgw(pA), writes=[("ss8", par)])
                    self.ts("dve", self.rs8[:, par, :], self.ss8[:, par, :], 1.0 / 64, EPS, ALU.mult, ALU.add,
                            [("ss8", par)], [("rs8", par)])
                    self.pow_rstd(self.rs8[:, par, :], self.rs8[:, par, :], 8, [("rs8", par)], [("rs8", par)])
                    qn = self.pf(pB)
                    self.tt("dve", qn.rearrange(h8, h=8), self.bkf(bk)[:, :].rearrange(h8, h=8),
                            self.rs8[:, par, :].unsqueeze(2).to_broadcast([128, 8, 64]), ALU.mult,
                            psk + [("rs8", par)], self.pgw(pB))
                    gain = self.gq if kind == "q" else self.gk
                    gkey = "gq" if kind == "q" else "gk"
                    gb_ = gain[:, :].unsqueeze(1).to_broadcast([128, 8, 64])
                    if kind == "q":
                        qb = self.pb(pC, 0)
                        self.tt("dve", qb.rearrange(h8, h=8), qn.rearrange(h8, h=8), gb_, ALU.mult,
                                self.pgw(pB) + [gkey], self.hk(pC, 0))
                        src, srck = qb, self.hk(pC, 0)
                    else:
                        kf = self.pf(pD)
                        self.tt("dve", kf.rearrange(h8, h=8), qn.rearrange(h8, h=8), gb_, ALU.mult,
                                self.pgw(pB) + [gkey], self.pgw(pD))
                        kb_ = self.pb(pC, 1)
                        self.cp("act", kb_, kf, self.pgw(pD), self.hk(pC, 1))
                        src, srck = kb_, self.hk(pC, 1)
                    def fin(src=src, srck=srck, kind=kind, b=b, gb=gb, samp=samp, par=par, pD=pD):
                        if kind == "k":
                            dst = self.ks if samp else self.kp[gb * 128:(gb + 1) * 128, :]
                            self.dma("sp", dst, self.pf(pD), self.pgw(pD), [], ("stk", par))
                        bk2 = self.nextbank()
                        for pr in range(4):
                            self.tr(self.bkb(bk2)[:, pr * 128:(pr + 1) * 128], src[:, pr * 128:(pr + 1) * 128], self.identb[:],
                                    srck + ["identb"], self.bkk(bk2))
                        tin = self.bkb(bk2)[:, 0:512].rearrange("p (r c) -> p r c", r=4)
                        if kind == "q":
                            for hl in range(2):
                                ps_ = slice(64 * hl, 64 * hl + 64)
                                self.cp("act", self.qz[ps_, hl, :, b * 128:(b + 1) * 128],
                                        self.bkb(bk2)[ps_, 0:512].rearrange("p (r c) -> p r c", r=4), self.bkk(bk2), [("qT", b)])
                        elif samp:
                            self.cp("act", self.kTs[:, :, :], tin, self.bkk(bk2), ["kTs"])
                        else:
                            self.cp("act", self.KT[:, :, gb * 128:(gb + 1) * 128], tin, self.bkk(bk2), [("KT", gb)])
                    if pending is not None:
                        pending()
                    pending = fin
                elif kind == "v":
                    vf = self.pf(pE)
                    self.cp("act", vf, self.bkf(bk)[:, :], psk, self.pgw(pE))
                    dst = self.vs if samp else self.vp[gb * 128:(gb + 1) * 128, :]
                    self.dma("sp", dst, vf, self.pgw(pE), [], ("stv", par))
                    if samp:
                        self.cp("dve", self.vsb[:, :], vf, self.pgw(pE), ["vsb"])
                    else:
                        self.cp("dve", self.V[:, gb, :], vf, self.pgw(pE), [("V", gb)])
                else:
                    self.cp("dve", self.vh[:, b, :], self.bkf(bk)[:, :], psk, [("vh", b)])
            if pending is not None:
                pending()
        for (cg, kind) in ((3, "hq"), (4, "hf"), (6, "hg")):
            r = self.use(ti, ("in", cg))
            w = self.wview(r, "cols")
            for hh in range(4):
                bk = self.nextbank()
                for k in range(8):
                    self.mm(self.bkf(bk)[:, 0:n], w[:, k, hh * 128:(hh + 1) * 128], self.hT[:, k, 0:n], k == 0, k == 7,
                            [("w", r)] + hk, self.bkk(bk))
                if kind == "hq":
                    self.act(self.qhT[:, hh, 0:n], self.bkf(bk)[:, 0:n], AF.Silu, self.bkk(bk), [("qhT", hh)])
                elif kind == "hf":
                    self.act(self.sgT[:, hh, 0:n], self.bkf(bk)[:, 0:n], AF.Tanh, self.bkk(bk), [("sgT", hh)], scale=0.5)
                else:
                    self.act(self.gT[:, hh, 0:n], self.bkf(bk)[:, 0:n], AF.Silu, self.bkk(bk), [("gT", hh)])

    def hgrn(self, T):
        nb, n, np_, gb0 = T["nb"], T["n"], T["np"], T["gb0"]
        npc = np_ * 128
        samp = T["samp"]
        nchp = np_ * 2
        c = self.cols
        tf, tk, tb_, teb, tenb = (self.pf(6 + i)[:, 0:n] for i in range(5))
        kf_, kk_, kb_k, keb, kenb = (self.pgw(6 + i) for i in range(5))
        c64 = "p (a b) -> p a b"
        for hh in range(4):
            qt_ = self.pb(hh, 0)[:, 0:n]
            kbt = self.pb(hh, 1)[:, 0:n]
            kht = self.pb(4 + hh // 2, hh % 2)[:, 0:n]
            kq0, kq1, kh = self.hk(hh, 0), self.hk(hh, 1), self.hk(4 + hh // 2, hh % 2)
            self.act(tf, self.sgT[:, hh, 0:n], AF.Ln, [("sgT", hh), "cols"], kf_,
                     scale=c[:, 28 + hh:29 + hh], bias=c[:, 24 + hh:25 + hh])
            self.ts("dve", tk, self.sgT[:, hh, 0:n], c[:, 34 + hh:35 + hh], c[:, 28 + hh:29 + hh], ALU.mult, ALU.add,
                    [("sgT", hh), "cols"], kk_)
            if npc:
                self.P.op("dve", lambda e: e.tensor_tensor_scan(out=tb_[:, 0:npc], data0=self.rmp[:, 0:npc], data1=tf[:, 0:npc],
                                                                initial=0.0, op0=ALU.mult, op1=ALU.add),
                          reads=kf_ + ["rmp"], writes=kb_k)
            if samp:
                self.P.op("dve", lambda e: e.tensor_tensor_scan(out=tb_[:, npc:n], data0=self.rms[:, :], data1=tf[:, npc:n],
                                                                initial=0.0, op0=ALU.mult, op1=ALU.add),
                          reads=kf_ + ["rms"], writes=kb_k)
            self.act(teb, tb_, AF.Exp, kb_k, keb)
            self.act(tenb, tb_, AF.Exp, kb_k, kenb, scale=-1.0)
            self.tt("dve", qt_, self.qhT[:, hh, 0:n], teb, ALU.mult, [("qhT", hh)] + keb, kq0)
            self.tt("dve", tk, tk, tenb, ALU.mult, kk_ + kenb, kk_)
            self.cp("act", kbt, tk, kk_, kq1)
            if npc:
                self.cp("dve", self.dec[:, hh, 0:nchp].unsqueeze(2),
                        teb[:, 0:npc].rearrange(c64, b=64)[:, :, 63:64], keb, [("dec", hh)])
                self.tt("dve", kht[:, 0:npc].rearrange(c64, b=64), tk[:, 0:npc].rearrange(c64, b=64),
                        self.dec[:, hh, 0:nchp].unsqueeze(2).to_broadcast([128, nchp, 64]), ALU.mult,
                        kk_ + [("dec", hh)], kh)
            if samp:
                self.cp("dve", self.decs[:, hh, 0:4].unsqueeze(2),
                        teb[:, npc:n].rearrange(c64, b=32)[:, :, 31:32], keb, [("decs", hh)])
                self.tt("dve", kht[:, npc:n].rearrange(c64, b=32), tk[:, npc:n].rearrange(c64, b=32),
                        self.decs[:, hh, 0:4].unsqueeze(2).to_broadcast([128, 4, 32]), ALU.mult,
                        kk_ + [("decs", hh)], kh)
        BPA, BPK = 0, 1
        for b in range(nb):
            issamp = samp and b == nb - 1
            C = 32 if issamp else 64
            nch = 128 // C
            bc = slice(b * 128, (b + 1) * 128)
            hm = self.hmask32 if issamp else self.hmask64
            hmk = "hmask32" if issamp else "hmask64"
            rm = self.rm32 if issamp else self.rm64
            gb = gb0 + b
            for hh in range(4):
                qt_ = self.pb(hh, 0)
                kbt = self.pb(hh, 1)
                self.mm(self.bkf(BPA)[:, hh * 128:(hh + 1) * 128], kbt[:, bc], qt_[:, bc], True, True,
                        self.hk(hh, 0) + self.hk(hh, 1), self.bkk(BPA))
            self.tt("dve", self.hbuf[:, :, 0, :], self.bkf(BPA)[:, :].rearrange("p (h t) -> p h t", h=4),
                    hm[:, :].unsqueeze(1).to_broadcast([128, 4, 128]), ALU.mult, self.bkk(BPA) + [hmk],
                    [("hb", hh, 0) for hh in range(4)])
            for hh in range(4):
                kht = self.pb(4 + hh // 2, hh % 2)
                self.tr(self.bkb(BPK)[:, hh * 128:(hh + 1) * 128], kht[:, bc], self.identb[:],
                        self.hk(4 + hh // 2, hh % 2) + ["identb"], self.bkk(BPK))
            for j in range(nch):
                self.ts("dve", self.hbuf[:, :, 1 + j, :], self.bkb(BPK)[:, 0:512].rearrange("p (h t) -> p h t", h=4),
                        rm[:, j:j + 1], None, ALU.mult, None, self.bkk(BPK) + ["rm%d" % C],
                        [("hb", hh, 1 + j) for hh in range(4)])
            if issamp:
                for hh in range(4):
                    qt_ = self.pb(hh, 0)
                    kq0 = self.hk(hh, 0)
                    po = self.bkf(2 + hh)
                    pok = self.bkk(2 + hh)
                    vsl = self.vh[:, b, hh * 128:(hh + 1) * 128]
                    stf = self.pf(6 + hh % 2).rearrange("p (s d) -> p s d", s=4)
                    stb = self.pb(8, hh % 2).rearrange("p (s d) -> p s d", s=4)
                    kstf, kstb = self.pgw(6 + hh % 2), self.hk(8, hh % 2)
                    self.dma("sp", stf, self.s0[:, hh, :, :].rearrange("s c d -> c s d"), [], kstf, ("lds", hh % 2))
                    self.cp("act", stb, stf, kstf, kstb)
                    for j in range(4):
                        cs = slice(b * 128 + 32 * j, b * 128 + 32 * j + 32)
                        self.mm(po[:, cs], vsl, self.hbuf[:, hh, 0, 32 * j:32 * j + 32], True, False,
                                [("vh", b), ("hb", hh, 0)], pok)
                        self.mm(po[:, cs], stb[:, j, :], qt_[:, cs], False, True, kstb + kq0, pok)
                    for j in range(4):
                        bu = 6 + j % 2
                        pu = self.bkf(bu)[:, 0:128]
                        self.mm(pu, self.hbuf[:, hh, 1 + j, :], vsl, True, True, [("hb", hh, 1 + j), ("vh", b)], self.bkk(bu))
                        self.stt(stf[:, j, :], stf[:, j, :], self.decs[:, hh, j:j + 1], pu, ALU.mult, ALU.add,
                                 self.bkk(bu) + [("decs", hh)] + kstf, kstf)
                    self.dma("pool", self.sts[:, hh, :, :].rearrange("s c d -> c s d"), stf, kstf, [], ("sts", hh % 2))
            else:
                for j in range(2):
                    for hh in range(4):
                        qt_ = self.pb(hh, 0)
                        kq0 = self.hk(hh, 0)
                        po = self.bkf(2 + hh)
                        pok = self.bkk(2 + hh)
                        vsl = self.vh[:, b, hh * 128:(hh + 1) * 128]
                        ci = 2 * b + j
                        cs = slice(b * 128 + 64 * j, b * 128 + 64 * j + 64)
                        par = ci % 2
                        noint = gb == 0 and j == 0
                        self.mm(po[:, cs], vsl, self.hbuf[:, hh, 0, 64 * j:64 * j + 64], True, noint,
                                [("vh", b), ("hb", hh, 0)], pok)
                        if not noint:
                            self.mm(po[:, cs], self.Sbf[:, hh, 1 - par, :], qt_[:, cs], False, True,
                                    [("Sbf", hh, 1 - par)] + kq0, pok)
                        bu = 6 + hh % 2
                        pu = self.bkf(bu)[:, 0:128]
                        self.mm(pu, self.hbuf[:, hh, 1 + j, :], vsl, True, True, [("hb", hh, 1 + j), ("vh", b)], self.bkk(bu))
                        self.stt(self.Sst[:, hh, :], self.Sst[:, hh, :], self.dec[:, hh, ci:ci + 1], pu, ALU.mult, ALU.add,
                                 self.bkk(bu) + [("dec", hh), ("S", hh)], [("S", hh)])
                        self.cp("act", self.Sbf[:, hh, par, :], self.Sst[:, hh, :], [("S", hh)], [("Sbf", hh, par)])
                if gb == self.npb - 1:
                    for hh in range(4):
                        self.dma("pool", self.stp[hh], self.Sst[:, hh, :], [("S", hh)], [], ("stp", hh))
        for hh in range(4):
            po = self.bkf(2 + hh)[:, 0:n]
            pok = self.bkk(2 + hh)
            sq = self.pb(9, hh % 2)[:, 0:n]
            sqk = self.hk(9, hh % 2)
            self.act(sq, po, AF.Square, pok, sqk)
            bn = hh % 2
            self.mm(self.bkf(bn)[:, 0:n], self.onesfu[:, :], sq, True, True, sqk + ["onesfu"], self.bkk(bn))
            ms = self.pf(10)[:, 0:n]
            self.act(ms, self.bkf(bn)[:, 0:n], AF.Ln, self.bkk(bn) + ["epsc"], self.pgw(10), scale=1.0 / 128, bias=self.epsc[:, 0:1])
            self.act(ms, ms, AF.Exp, self.pgw(10), self.pgw(10), scale=-0.5)
            self.stt(ms, po, self.cols[:, 32:33], ms, ALU.mult, ALU.mult, pok + ["cols"] + self.pgw(10), self.pgw(10))
            self.tt("dve", self.hT[:, 4 + hh, 0:n], ms, self.gT[:, hh, 0:n], ALU.mult, self.pgw(10) + [("gT", hh)],
                    [("hT", b) for b in range(nb)])

    def run_pipe(self, items, s1, s2, s3, s0=None):
        L = len(items)
        for step in range(L + 2):
            if 2 <= step and s0 is not None:
                s0(items[step - 2], step - 2)
            if step < L:
                s1(items[step], step)
            if 1 <= step <= L:
                s2(items[step - 1], step - 1)
            if 2 <= step:
                s3(items[step - 2], step - 2)

    def attention(self, T):
        nb, n, np_, gb0 = T["nb"], T["n"], T["np"], T["gb0"]
        npc = np_ * 128
        items = []
        for pr in range(4):
            kbs = list(range(gb0 + np_ - 1, -1, -1))
            for i, kb in enumerate(kbs):
                for hl in range(2):
                    items.append(dict(pr=pr, hl=hl, kb=kb, first=(i == 0), last=(i == len(kbs) - 1),
                                      c0=max(0, kb - gb0) * 128, diag=(kb >= gb0)))
        ZB, TB, CB, OB, NB_ = (0, 1), (2, 3), 4, (5, 6), 7
        qk = [("qT", b) for b in range(np_)]

        def s1(w, i):
            s = i % 3
            pr, hl, kb, c0 = w["pr"], w["hl"], w["kb"], w["c0"]
            zb = ZB[i % 2]
            z = self.bkf(zb)[:, c0:npc]
            self.mm(z, self.KT[:, pr, kb * 128:(kb + 1) * 128], self.qz[:, hl, pr, c0:npc], True, True,
                    [("KT", kb)] + qk, self.bkk(zb))
            e = self.pb(3 * s + 1, 0)[:, c0:npc]
            ek = self.hk(3 * s + 1, 0)
            self.act(e, z, AF.Exp, self.bkk(zb), ek, scale=SB_SCALE)
            if w["diag"]:
                ed = self.pb(3 * s + 1, 0)[:, c0:c0 + 128]
                self.tt("pool", ed, ed, self.cmaskA[:, :], ALU.mult, ek + ["cmaskA"], ek)
            sp = self.pb(3 * s + 2, 0)[:, c0:npc]
            self.act(sp, e, AF.Ln, ek, self.hk(3 * s + 2, 0), bias=1.0)

        def s2(w, i):
            s = i % 3
            hl, c0 = w["hl"], w["c0"]
            tb = TB[i % 2]
            Tb = self.bkf(tb)[:, c0:npc]
            sp = self.pb(3 * s + 2, 0)[:, c0:npc]
            spk = self.hk(3 * s + 2, 0)
            Rb = self.pf(9 + hl)
            rk = self.pgw(9 + hl)
            self.mm(Tb, self.triN[:, :], sp, True, True, ["triN"] + spk, self.bkk(tb))
            ex = self.pb(3 * s + 1, 1)[:, c0:npc]
            exk = self.hk(3 * s + 1, 1)
            if w["first"]:
                self.act(ex, Tb, AF.Exp, self.bkk(tb), exk)
            else:
                arg = self.pf(3 * s)[:, c0:npc]
                self.tt("dve", arg, Tb, Rb[:, c0:npc], ALU.add, self.bkk(tb) + rk, self.pgw(3 * s))
                self.act(ex, arg, AF.Exp, self.pgw(3 * s), exk)
            if not w["last"]:
                Cb = self.bkf(CB)[:, c0:npc]
                self.mm(Cb, self.onesN[:, :], sp, True, True, ["onesN"] + spk, self.bkk(CB))
                if w["first"]:
                    if c0 > 0:
                        self.P.op("dve", lambda e: e.memset(Rb[:, 0:c0], 0.0), writes=rk)
                    self.cp("dve", Rb[:, c0:npc], Cb, self.bkk(CB), rk)
                else:
                    self.tt("dve", Rb[:, c0:npc], Cb, Rb[:, c0:npc], ALU.add, self.bkk(CB) + rk, rk)

        def s3(w, i):
            s = i % 3
            pr, hl, kb, c0 = w["pr"], w["hl"], w["kb"], w["c0"]
            ob = OB[hl]
            O = self.bkf(ob)
            wT = self.pb(3 * s + 2, 1)[:, c0:npc]
            if w["first"]:
                self.mm(O[:, 0:npc], self.zeros[:, 0:128], self.zeros[:, 0:npc], True, False, ["zeros"], self.bkk(ob))
            self.mm(O[:, c0:npc], self.V[:, kb, pr * 128:(pr + 1) * 128], wT, False, w["last"],
                    [("V", kb)] + self.hk(3 * s + 2, 1), self.bkk(ob))
            if w["last"] and hl == 1:
                self.attn_norm2(npc, self.hT[:, pr, 0:npc], [("hT", b) for b in range(np_)], OB, NB_, s)

        def s0(w, i):
            s = i % 3
            c0 = w["c0"]
            self.tt("dve", self.pb(3 * s + 2, 1)[:, c0:npc], self.pb(3 * s + 1, 0)[:, c0:npc], self.pb(3 * s + 1, 1)[:, c0:npc],
                    ALU.mult, self.pgw(3 * s + 1), self.hk(3 * s + 2, 1))

        if np_ > 0:
            self.run_pipe(items, s1, s2, s3, s0)
        if T["samp"]:
            self.sample_attention(T)

    def attn_norm2(self, ncols, out, outk, OB, nbk, s):
        sq = self.pb(3 * s + 1, 0)[:, 0:ncols]
        sqk = self.hk(3 * s + 1, 0)
        for hl in range(2):
            ps_ = slice(64 * hl, 64 * hl + 64)
            self.act(sq[ps_, :], self.bkf(OB[hl])[ps_, 0:ncols], AF.Square, self.bkk(OB[hl]), sqk)
        self.mm(self.bkf(nbk)[:, 0:ncols], self.onesbd[:, :], sq, True, True, sqk + ["onesbd"], self.bkk(nbk))
        ms = self.pf(3 * s)[:, 0:ncols]
        msk = self.pgw(3 * s)
        self.act(ms, self.bkf(nbk)[:, 0:ncols], AF.Ln, self.bkk(nbk) + ["epsc"], msk, scale=1.0 / 64, bias=self.epsc[:, 0:1])
        self.act(ms, ms, AF.Exp, msk, msk, scale=-0.5)
        for hl in range(2):
            ps_ = slice(64 * hl, 64 * hl + 64)
            self.stt(out[ps_, :], self.bkf(OB[hl])[ps_, 0:ncols], self.cols[ps_, 33:34], ms[ps_, :], ALU.mult, ALU.mult,
                     self.bkk(OB[hl]) + ["cols"] + msk, outk)

    def attn_norm(self, O, ob, out, outk, ncols, nbk, s, out3=None):
        assert ncols <= 256
        sq = self.pb(3 * s + 1, 0)[:, 0:ncols]
        sqk = self.hk(3 * s + 1, 0)
        self.act(sq, O, AF.Square, self.bkk(ob), sqk)
        self.mm(self.bkf(nbk)[:, 0:ncols], self.onesbd[:, :], sq, True, True, sqk + ["onesbd"], self.bkk(nbk))
        ms = self.pf(3 * s)[:, 0:ncols]
        msk = self.hk(3 * s, 0)
        self.act(ms, self.bkf(nbk)[:, 0:ncols], AF.Ln, self.bkk(nbk) + ["epsc"], msk, scale=1.0 / 64, bias=self.epsc[:, 0:1])
        self.act(ms, ms, AF.Exp, msk, msk, scale=-0.5)
        if out3 is None:
            self.stt(out, O, self.cols[:, 33:34], ms, ALU.mult, ALU.mult, self.bkk(ob) + ["cols"] + msk, outk)
        else:
            self.stt(ms, O, self.cols[:, 33:34], ms, ALU.mult, ALU.mult, self.bkk(ob) + ["cols"] + msk, msk)
            self.cp("dve", out3, ms.rearrange("p (r t) -> p r t", r=4), msk, outk)

    def sample_attention(self, T):
        nb, np_ = T["nb"], T["np"]
        npc = np_ * 128
        PB = self.pastb
        ZB, TB, CB, OB, NB_ = (0, 1), (2, 3), 4, (5, 6), 7
        qkey = [("qT", nb - 1)]
        STG = [(self.pb(0, 1), self.hk(0, 1)), (self.pb(3, 1), self.hk(3, 1)), (self.pb(6, 1), self.hk(6, 1)),
               (self.pb(10, 0), self.hk(10, 0)), (self.pb(10, 1), self.hk(10, 1)), (self.pb(9, 0), self.hk(9, 0))]
        self.ldn = 0
        fifo = []
        LAG = 4

        def finish_oldest():
            i, g = fifo.pop(0)
            st, stk = STG[i % 6]
            bk = NB_ if i % 2 == 0 else CB
            for pr in range(4):
                self.tr(self.bkb(bk)[:, pr * 128:(pr + 1) * 128], st[:, pr * 128:(pr + 1) * 128], self.identb[:],
                        stk + ["identb"], self.bkk(bk))
            tin = self.bkb(bk)[:, 0:512].rearrange("p (r c) -> p r c", r=4)
            self.cp("act" if i % 2 == 0 else "dve", self.KT[:, :, g * 128:(g + 1) * 128], tin, self.bkk(bk), [("KT", g)])

        def load_blocks(j, g):
            i = self.ldn
            self.ldn += 1
            self.dma("pool", self.V[:, g, :], self.cv[j, g * 128:(g + 1) * 128, :], [], [("V", g)], ("ldcv", i % 4))
            st, stk = STG[i % 6]
            self.dma("pool", st, self.ck[j, g * 128:(g + 1) * 128, :], [], stk, ("ldck", i % 6))
            fifo.append((i, g))
            if len(fifo) > LAG:
                finish_oldest()

        for hl in range(2):
            ps_ = slice(64 * hl, 64 * hl + 64)
            self.cp("dve", self.qbd[ps_, :, :, 32 * hl:32 * hl + 32],
                    self.qz[ps_, hl, :, npc:npc + 128].rearrange("p r (j t) -> p r j t", j=4), qkey, ["qbd"])

        def kv(w):
            if w["kb"] == "new":
                return (lambda pr: self.kTs[:, pr, :]), ["kTs"], (lambda h: self.vsb[:, h * 64:(h + 1) * 64]), ["vsb"]
            kb = w["kb"]
            return ((lambda pr: self.KT[:, pr, kb * 128:(kb + 1) * 128]), [("KT", kb)],
                    (lambda h: self.V[:, kb, h * 64:(h + 1) * 64]), [("V", kb)])

        def s1(w, i):
            s = i % 3
            j = w["j"]
            kf, kk, vf, vk = kv(w)
            zb = ZB[i % 2]
            z = self.bkf(zb)[:, 0:256]
            for pr in range(4):
                self.mm(z[:, pr * 64:(pr + 1) * 64], kf(pr), self.qbd[:, pr, j, :], True, True,
                        kk + ["qbd"], self.bkk(zb))
            e = self.pb(3 * s + 1, 0)[:, 0:256]
            ek = self.hk(3 * s + 1, 0)
            self.act(e, z, AF.Exp, self.bkk(zb), ek, scale=SB_SCALE)
            if w["kb"] == "new":
                self.tt("pool", e, e, self.smask[:, j, :], ALU.mult, ek + ["smask"], ek)
            sp = self.pb(3 * s + 2, 0)[:, 0:256]
            self.act(sp, e, AF.Ln, ek, self.hk(3 * s + 2, 0), bias=1.0)

        def s2(w, i):
            s = i % 3
            tb = TB[i % 2]
            Tb = self.bkf(tb)[:, 0:256]
            sp = self.pb(3 * s + 2, 0)[:, 0:256]
            spk = self.hk(3 * s + 2, 0)
            Rb = self.pf(9)[:, 256:512]
            rk = self.hk(9, 1)
            self.mm(Tb, self.triN[:, :], sp, True, True, ["triN"] + spk, self.bkk(tb))
            ex = self.pb(3 * s + 1, 1)[:, 0:256]
            exk = self.hk(3 * s + 1, 1)
            if w["first"]:
                self.act(ex, Tb, AF.Exp, self.bkk(tb), exk)
            else:
                arg = self.pf(3 * s)[:, 0:256]
                self.tt("dve", arg, Tb, Rb, ALU.add, self.bkk(tb) + rk, self.hk(3 * s, 0))
                self.act(ex, arg, AF.Exp, self.hk(3 * s, 0), exk)
            if not w["last"]:
                Cb = self.bkf(tb)[:, 256:512]
                self.mm(Cb, self.onesN[:, :], sp, True, True, ["onesN"] + spk, self.bkk(tb))
                if w["first"]:
                    self.cp("dve", Rb, Cb, self.bkk(tb), rk)
                else:
                    self.tt("dve", Rb, Cb, Rb, ALU.add, self.bkk(tb) + rk, rk)

        def s0(w, i):
            s = i % 3
            self.tt("dve", self.pb(3 * s + 2, 1)[:, 0:256], self.pb(3 * s + 1, 0)[:, 0:256], self.pb(3 * s + 1, 1)[:, 0:256],
                    ALU.mult, self.pgw(3 * s + 1), self.hk(3 * s + 2, 1))

        def s3(w, i):
            s = i % 3
            j = w["j"]
            kf, kk, vf, vk = kv(w)
            ob = OB[j % 2]
            O = self.bkf(ob)
            if w["first"]:
                self.mm(O[:, 0:128], self.zeros[:, 0:128], self.zeros[:, 0:128], True, False, ["zeros"], self.bkk(ob))
            wT = self.pb(3 * s + 2, 1)
            for h in range(8):
                pr, hl = h // 2, h % 2
                self.mm(O[64 * hl:64 * hl + 64, pr * 32:(pr + 1) * 32], vf(h), wT[:, h * 32:(h + 1) * 32], False,
                        w["last"] and h >= 6, vk + self.hk(3 * s + 2, 1), self.bkk(ob))
            if w["last"]:
                q0 = npc + 32 * j
                self.attn_norm(O[:, 0:128], ob, None, [("hT", nb - 1)], 128, NB_, s, out3=self.hT[:, 0:4, q0:q0 + 32])
            if w["kb"] != "new" and j < 3:
                load_blocks(j + 1, w["kb"])
                if j + 1 == 3 and w["kb"] == 0:
                    while fifo:
                        finish_oldest()

        assert PB >= 8, "interleaved cache reload needs the next sequence's first reader well behind the staged transposes"
        for g0 in range(PB - 1, -1, -1):
            load_blocks(0, g0)
        while fifo:
            finish_oldest()
        items = []
        for j in range(4):
            kbs = ["new"] + list(range(PB - 1, -1, -1))
            items += [dict(j=j, kb=kb, first=(i == 0), last=(i == len(kbs) - 1)) for i, kb in enumerate(kbs)]
        self.run_pipe(items, s1, s2, s3, s0)
        assert not fifo

    def outproj(self, T):
        ti, nb = T["ti"], T["nb"]
        rs = [self.use(ti, ("out", 0)), self.use(ti, ("out", 1), oldest=("out", 0))]
        for b in range(nb):
            for nh in range(2):
                bk = self.nextbank()
                for k in range(8):
                    r = rs[k // 4]
                    w = self.wview(r, "rows")
                    self.mm(self.bkf(bk)[:, :], self.hT[:, k, b * 128:(b + 1) * 128], w[:, k % 4, nh * 512:(nh + 1) * 512],
                            k == 0, k == 7, [("w", r), ("hT", b)], self.bkk(bk))
                xs_ = self.xt[:, b, nh * 512:(nh + 1) * 512]
                self.tt("dve", xs_, self.bkf(bk)[:, :], xs_, ALU.add, self.bkk(bk) + [("xt", b)], [("xt", b)])

    def tile(self, ti):
        gb0, np_, samp = self.tiles[ti]
        nb = np_ + (1 if samp else 0)
        T = dict(ti=ti, gb0=gb0, np=np_, samp=samp, nb=nb, n=nb * 128)
        for b in range(nb):
            if samp and b == nb - 1:
                src = self.xs
            else:
                src = self.xp[(gb0 + b) * 128:(gb0 + b + 1) * 128, :]
            self.dma("sp", self.xt[:, b, :], src, [], [("xt", b)], ("ldx", b))
        self.ffn(T, 1)
        self.inproj(T)
        self.hgrn(T)
        self.attention(T)
        self.outproj(T)
        self.ffn(T, 2)
        for b in range(nb):
            if samp and b == nb - 1:
                dst = self.ys
            else:
                dst = self.yp[(gb0 + b) * 128:(gb0 + b + 1) * 128, :]
            self.dma("sp", dst, self.xt[:, b, :], [("xt", b)], [], ("sty", b))

    def build(self):
        nc = self.nc
        with contextlib.ExitStack() as es:
            self.declare(es)
            self.setup()
            for ti in range(len(self.tiles)):
                self.tile(ti)
            esem = {e: es.enter_context(nc.semaphore("s_" + e)) for e in ENGS}
            dsem = {}
            for i, k in enumerate(self.P.dcount.keys()):
                dsem[k] = es.enter_context(nc.semaphore("d%d" % i))
            with nc.Block() as block:
                self.P.emit(block, esem, dsem)
        return nc


_CACHE = {}


def _get_nc(npb, pastb):
    key = (npb, pastb)
    if key not in _CACHE:
        _CACHE[key] = Builder(npb, pastb).build()
    return _CACHE[key]


def kernel(x_prompt, x_sample, cache_sb_k, cache_sb_v, state_hgrn,
           ffn1_norm, ffn1_w_in, ffn1_w_out, mix_norm, w_in, sb_q_gain, sb_k_gain,
           hg_lb_logits, sb_out_gain, hg_out_gain, w_out, ffn2_norm, ffn2_w_in, ffn2_w_out):
    f = lambda a: np.ascontiguousarray(np.asarray(a, dtype=np.float32))
    x_prompt, x_sample = f(x_prompt), f(x_sample)
    cache_sb_k, cache_sb_v, state_hgrn = f(cache_sb_k), f(cache_sb_v), f(state_hgrn)
    B, S, _ = x_prompt.shape
    DB, DS, _ = x_sample.shape
    PAST = cache_sb_k.shape[2]
    assert B == 8 and DB == 32 and DS == 32 and S % 128 == 0 and PAST % 128 == 0
    npb, pastb = S // 128, PAST // 128
    nc = _get_nc(npb, pastb)
    vrows = np.concatenate([
        f(ffn1_norm)[0].reshape(8, 128), f(mix_norm)[0].reshape(8, 128), f(ffn2_norm)[0].reshape(8, 128),
        f(hg_lb_logits)[0].reshape(4, 128), f(hg_lb_logits)[1].reshape(4, 128),
        f(hg_out_gain)[0].reshape(1, 128), np.tile(f(sb_out_gain)[0], 2).reshape(1, 128)], axis=0)
    qkg = np.stack([f(sb_q_gain)[0], f(sb_k_gain)[0]], axis=0)
    shared = {"f1wi": f(ffn1_w_in)[0], "f1wo": f(ffn1_w_out)[0], "win": f(w_in)[0], "wout": f(w_out)[0],
              "f2wi": f(ffn2_w_in)[0], "f2wo": f(ffn2_w_out)[0], "vrows": np.ascontiguousarray(vrows),
              "qkg": np.ascontiguousarray(qkg)}
    in_maps = []
    for c in range(8):
        m = dict(shared)
        m["xp"] = x_prompt[c]
        m["xs"] = np.ascontiguousarray(x_sample[4 * c:4 * c + 4].reshape(128, D))
        m["ck"] = np.ascontiguousarray(cache_sb_k[0, 4 * c:4 * c + 4].reshape(4, PAST, 512))
        m["cv"] = np.ascontiguousarray(cache_sb_v[0, 4 * c:4 * c + 4].reshape(4, PAST, 512))
        m["s0"] = np.ascontiguousarray(state_hgrn[0, 4 * c:4 * c + 4])
        in_maps.append(m)
    res = run_bass_kernel_spmd(nc, in_maps, core_ids=list(range(8)))
    R = res.results
    y_prompt = np.stack([R[c]["yp"] for c in range(8)], axis=0)
    y_sample = np.concatenate([R[c]["ys"].reshape(4, 32, D) for c in range(8)], axis=0)
    kp = np.stack([R[c]["kp"].reshape(S, 8, 64) for c in range(8)], axis=0)[None]
    vp = np.stack([R[c]["vp"].reshape(S, 8, 64) for c in range(8)], axis=0)[None]
    stp = np.stack([R[c]["stp"] for c in range(8)], axis=0)[None]
    ks = np.concatenate([R[c]["ks"].reshape(4, 32, 8, 64) for c in range(8)], axis=0)[None]
    vs = np.concatenate([R[c]["vs"].reshape(4, 32, 8, 64) for c in range(8)], axis=0)[None]
    sts = np.concatenate([R[c]["sts"] for c in range(8)], axis=0)[None]
    outs = (y_prompt, y_sample, kp, vp, stp, ks, vs, sts)
    return tuple(np.ascontiguousarray(o, dtype=np.float32) for o in outs)
```
